# Optimizing a Trainium2 kernel written in Bass

```python
import jax, jax.numpy as jnp
from jax import lax
import numpy as np

D_MODEL = 1024
BATCH = 4
SEQ = 8192
DEPTH = 1

CHUNK = 64
N_PREV_CHUNKS = 8
BAND = (N_PREV_CHUNKS + 1) * CHUNK
ATT_HEADS = 8
ATT_HEAD_DIM = 64
ATT_WIDTH = ATT_HEADS * ATT_HEAD_DIM
REL_CLIP = 128
SG_BLOCK = 128
SG_GROUPS = 8
SG_GROUP_DIM = 64
SG_WIDTH = SG_GROUPS * SG_GROUP_DIM
N_BRANCHES = 2
IN_COLS = 3 * ATT_WIDTH + 2 * SG_WIDTH + N_BRANCHES * D_MODEL
MEM_LEN = 256
XATT_HEADS = 4
XATT_HEAD_DIM = D_MODEL // XATT_HEADS
D_FF = -(-8 * D_MODEL // (3 * 256)) * 256
EPS = 1e-6
NEG_INF = -1e30

kernel_name = "hybrid_chunk_attn_sgu_block"


def rmsnorm(x, g):
    xf = x.astype(jnp.float32)
    y = xf * lax.rsqrt(jnp.mean(xf * xf, axis=-1, keepdims=True) + EPS)
    return (y * g.astype(jnp.float32)).astype(x.dtype)


def layernorm(x, g, b):
    xf = x.astype(jnp.float32)
    mu = jnp.mean(xf, axis=-1, keepdims=True)
    var = jnp.mean(jnp.square(xf - mu), axis=-1, keepdims=True)
    y = (xf - mu) * lax.rsqrt(var + EPS)
    return (y * g.astype(jnp.float32) + b.astype(jnp.float32)).astype(x.dtype)


def chunked_relpos_attention(q, k, v, rel_bias):
    B, S, H, Dh = q.shape
    nC = S // CHUNK
    q = (q * (Dh ** -0.5)).reshape(B, nC, CHUNK, H, Dh)
    k = k.reshape(B, nC, CHUNK, H, Dh)
    v = v.reshape(B, nC, CHUNK, H, Dh)
    pad = ((0, 0), (N_PREV_CHUNKS, 0), (0, 0), (0, 0), (0, 0))
    kp = jnp.pad(k, pad)
    vp = jnp.pad(v, pad)
    kb = jnp.stack([kp[:, j:j + nC] for j in range(N_PREV_CHUNKS + 1)], axis=2).reshape(B, nC, BAND, H, Dh)
    vb = jnp.stack([vp[:, j:j + nC] for j in range(N_PREV_CHUNKS + 1)], axis=2).reshape(B, nC, BAND, H, Dh)
    s = jnp.einsum('bcihd,bcmhd->bhcim', q, kb, preferred_element_type=jnp.float32)
    qi = np.arange(CHUNK)[:, None]
    mi = np.arange(BAND)[None, :]
    dist = qi - mi + N_PREV_CHUNKS * CHUNK
    idx = np.clip(dist, -REL_CLIP, REL_CLIP) + REL_CLIP
    bias = rel_bias[:, idx].astype(jnp.float32)
    s = s + bias[None, :, None]
    valid = (np.arange(nC)[:, None] - N_PREV_CHUNKS + np.arange(BAND)[None, :] // CHUNK) >= 0
    s = jnp.where(valid[None, None, :, None, :], s, NEG_INF)
    p = jax.nn.softmax(s, axis=-1)
    o = jnp.einsum('bhcim,bcmhd->bcihd', p.astype(vb.dtype), vb)
    return o.reshape(B, S, H * Dh)


def spatial_gating(u, v, ln_g, ln_b, w_s, b_s):
    B, S, _ = u.shape
    nB = S // SG_BLOCK
    v = v.reshape(B, nB, SG_BLOCK, SG_GROUPS, SG_GROUP_DIM)
    v = layernorm(v, ln_g, ln_b)
    t = np.arange(SG_BLOCK)
    mask = (t[None, :] // CHUNK) <= (t[:, None] // CHUNK)
    w = jnp.where(mask[None], w_s, 0.0)
    sv = jnp.einsum('gts,bnsgd->bntgd', w, v) + b_s.T[None, None, :, :, None]
    return (u.reshape(B, nB, SG_BLOCK, SG_GROUPS, SG_GROUP_DIM) * sv).reshape(B, S, SG_WIDTH)


def cross_attention(h, m, w_xq, w_xkv, w_xo):
    B, S, _ = h.shape
    M = m.shape[1]
    q = (h @ w_xq).reshape(B, S, XATT_HEADS, XATT_HEAD_DIM) * (XATT_HEAD_DIM ** -0.5)
    k, v = jnp.split(m @ w_xkv, 2, axis=-1)
    k = k.reshape(B, M, XATT_HEADS, XATT_HEAD_DIM)
    v = v.reshape(B, M, XATT_HEADS, XATT_HEAD_DIM)
    s = jnp.einsum('bshd,bmhd->bhsm', q, k, preferred_element_type=jnp.float32)
    p = jax.nn.softmax(s, axis=-1)
    o = jnp.einsum('bhsm,bmhd->bshd', p.astype(v.dtype), v).reshape(B, S, D_MODEL)
    return o @ w_xo


def setup_inputs(seed: int = 0) -> dict:
    key = jax.random.key(seed)
    ks = jax.random.split(key, 24)
    f32 = jnp.float32
    nrm = lambda k, shape, scale: jax.random.normal(k, shape, f32) * scale
    L = DEPTH
    return {
        "x": nrm(ks[0], (BATCH, SEQ, D_MODEL), 1.0),
        "mem": nrm(ks[1], (BATCH, MEM_LEN, D_MODEL), 1.0),
        "norm_mix_g": 1.0 + nrm(ks[2], (L, D_MODEL), 0.02),
        "w_in": nrm(ks[3], (L, D_MODEL, IN_COLS), D_MODEL ** -0.5),
        "rel_bias": nrm(ks[4], (L, ATT_HEADS, 2 * REL_CLIP + 1), 0.5),
        "sg_ln_g": 1.0 + nrm(ks[5], (L, SG_GROUPS, SG_GROUP_DIM), 0.02),
        "sg_ln_b": nrm(ks[6], (L, SG_GROUPS, SG_GROUP_DIM), 0.02),
        "sg_w": nrm(ks[7], (L, SG_GROUPS, SG_BLOCK, SG_BLOCK), SG_BLOCK ** -0.5),
        "sg_b": 1.0 + nrm(ks[8], (L, SG_GROUPS, SG_BLOCK), 0.02),
        "w_branch_att": nrm(ks[9], (L, ATT_WIDTH, D_MODEL), ATT_WIDTH ** -0.5),
        "w_branch_sg": nrm(ks[10], (L, SG_WIDTH, D_MODEL), SG_WIDTH ** -0.5),
        "w_out": nrm(ks[11], (L, D_MODEL, D_MODEL), D_MODEL ** -0.5),
        "norm_xattn_g": 1.0 + nrm(ks[12], (L, D_MODEL), 0.02),
        "norm_mem_g": 1.0 + nrm(ks[13], (L, D_MODEL), 0.02),
        "w_xq": nrm(ks[14], (L, D_MODEL, D_MODEL), D_MODEL ** -0.5),
        "w_xkv": nrm(ks[15], (L, D_MODEL, 2 * D_MODEL), D_MODEL ** -0.5),
        "w_xo": nrm(ks[16], (L, D_MODEL, D_MODEL), D_MODEL ** -0.5),
        "norm_ffn_g": 1.0 + nrm(ks[17], (L, D_MODEL), 0.02),
        "w_ffn_in": nrm(ks[18], (L, D_MODEL, 2 * D_FF), D_MODEL ** -0.5),
        "w_ffn_out": nrm(ks[19], (L, D_FF, D_MODEL), D_FF ** -0.5),
        "norm_final_g": 1.0 + nrm(ks[20], (D_MODEL,), 0.02),
    }


def reference(x, mem, norm_mix_g, w_in, rel_bias, sg_ln_g, sg_ln_b, sg_w, sg_b,
              w_branch_att, w_branch_sg, w_out, norm_xattn_g, norm_mem_g,
              w_xq, w_xkv, w_xo, norm_ffn_g, w_ffn_in, w_ffn_out, norm_final_g):
    B, S, _ = x.shape
    col = np.cumsum([ATT_WIDTH, ATT_WIDTH, ATT_WIDTH, SG_WIDTH, SG_WIDTH, D_MODEL])
    for l in range(DEPTH):
        h = rmsnorm(x, norm_mix_g[l])
        z = h @ w_in[l]
        q, k, v, u_sg, v_sg, g_a, g_b = jnp.split(z, col, axis=-1)
        q = q.reshape(B, S, ATT_HEADS, ATT_HEAD_DIM)
        k = k.reshape(B, S, ATT_HEADS, ATT_HEAD_DIM)
        v = v.reshape(B, S, ATT_HEADS, ATT_HEAD_DIM)
        y_att = chunked_relpos_attention(q, k, v, rel_bias[l])
        y_sg = spatial_gating(jax.nn.gelu(u_sg), jax.nn.gelu(v_sg),
                              sg_ln_g[l], sg_ln_b[l], sg_w[l], sg_b[l])
        merged = (jax.nn.sigmoid(g_a) * (y_att @ w_branch_att[l])
                  + jax.nn.sigmoid(g_b) * (y_sg @ w_branch_sg[l]))
        x = x + merged @ w_out[l]
        x = x + cross_attention(rmsnorm(x, norm_xattn_g[l]), rmsnorm(mem, norm_mem_g[l]),
                                w_xq[l], w_xkv[l], w_xo[l])
        gate, up = jnp.split(rmsnorm(x, norm_ffn_g[l]) @ w_ffn_in[l], 2, axis=-1)
        x = x + (jax.nn.silu(gate) * up) @ w_ffn_out[l]
    return rmsnorm(x, norm_final_g)
```

```python
import contextlib
import numpy as np
import concourse.bass as bass
import concourse.mybir as mybir
from concourse.bass_utils import run_bass_kernel_spmd

F32 = mybir.dt.float32
BF16 = mybir.dt.bfloat16
AF = mybir.ActivationFunctionType
ALU = mybir.AluOpType
AX = mybir.AxisListType

NCORES = 8
D = 1024
T = 512
NPASS = 8
TOK = T * NPASS
HALO = 512
DFF = 2816
EPS = 1e-6
NSLOT = 6
SEM_EPOCH = 6000
OWN_WAIT = True
STOP = None


class _Stop(Exception):
    pass


class Buf:
    __slots__ = ("name", "w", "r", "al", "d")

    def __init__(self, name):
        self.name = name
        self.w = None
        self.r = {}
        self.al = []
        self.d = {}


def alias(ga, gb):
    for b in ga:
        for c in gb:
            if c not in b.al:
                b.al.append(c)
            if b not in c.al:
                c.al.append(b)


class _PEProxy:
    def __init__(self, pe):
        self._pe = pe
        self.n = 0

    def matmul(self, *a, **k):
        self.n += 1
        return self._pe.matmul(*a, **k)

    def transpose(self, *a, **k):
        self.n += 1
        return self._pe.transpose(*a, **k)

    def wait_ge(self, *a, **k):
        return self._pe.wait_ge(*a, **k)


MARKS = []


class Sched:
    def __init__(self, nc, es):
        self.nc = nc
        self.es = es
        self.eng = {"pe": _PEProxy(nc.tensor), "act": nc.scalar, "dve": nc.vector, "pool": nc.gpsimd, "sp": nc.sync}
        self.sem = {}
        self.cnt = {}
        self.key = {}
        self.known = {k: {} for k in self.eng}
        self.nsem = 0
        for k in self.eng:
            self._new_sem(k)

    def _mk(self, name):
        self.nsem += 1
        return self.es.enter_context(self.nc.semaphore(f"{name}_{self.nsem}"))

    def _new_sem(self, k):
        self.sem[k] = self._mk("e" + k)
        self.cnt[k] = 0
        self.key[k] = (k, self.nsem)

    def _deps(self, reads, writes):
        deps = {}

        def add(ev):
            if ev is None:
                return
            key, h, v = ev
            if key not in deps or deps[key][1] < v:
                deps[key] = (h, v)

        for b in reads:
            add(b.w)
        for b in writes:
            add(b.w)
            for ev in b.r.values():
                add(ev)
            for c in b.al:
                add(c.w)
                for ev in c.r.values():
                    add(ev)
        return deps

    def _wait(self, k, deps):
        e = self.eng[k]
        kn = self.known[k]
        for key, (h, v) in deps.items():
            if key == self.key[k] and (k == "pe" or not OWN_WAIT):
                continue
            if kn.get(key, 0) >= v:
                continue
            e.wait_ge(h, v)
            kn[key] = v

    def _record(self, ev, reads, writes):
        for b in writes:
            b.w = ev
            b.r = {}
        for b in reads:
            if b in writes:
                continue
            b.r[ev[0]] = ev

    def op(self, k, reads, writes, fn):
        self._wait(k, self._deps(reads, writes))
        ins = fn(self.eng[k])
        if self.cnt[k] >= SEM_EPOCH:
            self._new_sem(k)
        self.cnt[k] += 1
        ins.then_inc(self.sem[k], 1)
        ev = (self.key[k], self.sem[k], self.cnt[k])
        self._record(ev, reads, writes)

    def dma(self, k, sbuf, reads, writes, fn):
        if k not in sbuf.d:
            sbuf.d[k] = [self._mk("d" + k + sbuf.name), 0, ("dma", self.nsem)]
        d = sbuf.d[k]
        self._wait(k, self._deps(reads, writes))
        inss = fn(self.eng[k])
        if not isinstance(inss, (list, tuple)):
            inss = [inss]
        for ins in inss:
            ins.then_inc(d[0], 16)
            d[1] += 16
        ev = (d[2], d[0], d[1])
        self._record(ev, reads, writes)
        return ev


def build_program():
    nc = bass.Bass("TRN2", target_bir_lowering=False)
    es = contextlib.ExitStack()

    def din(name, shape):
        return nc.dram_tensor(name, list(shape), F32, kind="ExternalInput").ap()

    xc = din("xc", [HALO + TOK, D])
    memc = din("memc", [256, D])
    hflag = din("hflag", [128, 8])
    w_in = din("w_in", [D, 4608])
    w_ba = din("w_ba", [512, D])
    w_bs = din("w_bs", [512, D])
    w_out = din("w_out", [D, D])
    w_xq = din("w_xq", [D, D])
    w_xkv = din("w_xkv", [D, 2 * D])
    w_xo = din("w_xo", [D, D])
    w_fi = din("w_fi", [D, 2 * DFF])
    w_fo = din("w_fo", [DFF, D])
    gT_d = din("gT", [128, 32])
    gfin_d = din("gfin", [D])
    b3_d = din("b3", [128, 1024])
    b4_d = din("b4", [128, 1024])
    bfar_d = din("bfar", [128, 1024])
    mask4_d = din("mask4", [128, 1024])
    sgwT_d = din("sgwT", [128, 1024])
    sgmT_d = din("sgmT", [128, 1024])
    sgwN_d = din("sgwN", [128, 1024])
    sgmN_d = din("sgmN", [128, 1024])
    sgbT_d = din("sgbT", [128, 8])
    lng_d = din("lng", [512])
    lnb_d = din("lnb", [512])
    ident_d = din("ident", [128, 128])
    out = nc.dram_tensor("out", [TOK, D], F32, kind="ExternalOutput").ap()
    NUNIT = 40
    wscr = nc.dram_tensor("wscr", [NUNIT, 128, 4096], BF16, kind="Internal").ap()

    S = Sched(nc, es)

    def sb(name, shape, dt):
        return es.enter_context(nc.sbuf_tensor("s_" + name, list(shape), dt))

    with es:
        ring = [sb(f"ring{i}", [128, 4096], BF16) for i in range(NSLOT)]
        ringB = [Buf(f"ring{i}") for i in range(NSLOT)]
        xs_sets = [[sb(f"xs{j}_{i}", [128, D], F32) for i in range(4)] for j in range(2)]
        xsB_sets = [[Buf(f"xs{j}_{i}") for i in range(4)] for j in range(2)]
        xs = list(xs_sets[0])
        xsB = list(xsB_sets[0])

        def use_set(j):
            xs[:] = xs_sets[j]
            xsB[:] = xsB_sets[j]
        hT = sb("hT", [128, 8 * T], BF16)
        hT3 = hT[:].rearrange("p (k t) -> p k t", k=8)
        hTB = [Buf(f"hT{i}") for i in range(4)]
        hn = [sb(f"hn{i}", [128, D], BF16) for i in range(2)]
        hnB = [Buf(f"hn{i}") for i in range(2)]
        junk = sb("junk", [128, D], BF16)
        junkB = Buf("junk")
        st_ss = sb("st_ss", [128, 8], F32)
        st_sd = sb("st_sd", [128, 8], F32)
        st_rs = sb("st_rs", [128, 8], F32)
        ssB = [Buf(f"ss{i}") for i in range(8)]
        sdB = [Buf(f"sd{i}") for i in range(8)]
        rsB = [Buf(f"rs{i}") for i in range(8)]
        qT = sb("qT", [128, 4 * T], BF16)
        qT3 = qT[:].rearrange("p (a t) -> p a t", a=4)
        qTB = [Buf(f"qT{i}") for i in range(4)]
        kT = sb("kT", [128, 4 * 2 * T], BF16)
        kT3 = kT[:].rearrange("p (a t) -> p a t", a=4)
        kTB = [[Buf(f"kT{h}_{a}") for a in range(4)] for h in range(2)]
        vA = sb("vA", [128, 8 * 528], BF16)
        vA3 = vA[:].rearrange("p (s c) -> p s c", s=8)
        vAB = [Buf(f"vA{i}") for i in range(8)]
        arena = sb("arena", [128, 13312], BF16)
        gu = [arena[:, 5120 + 1024 * i: 5120 + 1024 * (i + 1)].bitcast(F32) for i in range(4)]
        guB = [Buf(f"gu{i}") for i in range(4)]
        gv = [sb(f"gv{i}", [128, 512], F32) for i in range(2)]
        gvB = [Buf(f"gv{i}") for i in range(2)]
        sq = sb("sq", [128, 512], F32)
        sqB = Buf("sq")
        lst = sb("lst", [128, 64], F32)
        lstB = [Buf(f"lst{i}") for i in range(2)]
        vn = [sb(f"vn{i}", [128, 512], BF16) for i in range(2)]
        vnB = [Buf(f"vn{i}") for i in range(2)]
        pT = [arena[:, 2560 * i: 2560 * (i + 1)] for i in range(2)]
        pT3 = [t.rearrange("p (j c) -> p j c", j=5) for t in pT]
        pTB = [[Buf(f"pT{i}_{j}") for j in range(5)] for i in range(2)]
        rc = sb("rc", [128, 8], F32)
        rcB = [Buf(f"rc{i}") for i in range(2)]
        yatt = [sb(f"yatt{i}", [128, 512], BF16) for i in range(2)]
        yattB = [Buf(f"yatt{i}") for i in range(2)]
        ysg = [sb(f"ysg{i}", [128, 512], BF16) for i in range(2)]
        ysgB = [Buf(f"ysg{i}") for i in range(2)]
        svt, svtB = gv, gvB
        yTa = sb("yTa", [128, 4 * T], BF16)
        yTa3 = yTa[:].rearrange("p (k t) -> p k t", k=4)
        yTaB = [Buf(f"yTa{i}") for i in range(4)]
        yTs = sb("yTs", [128, 4 * T], BF16)
        yTs3 = yTs[:].rearrange("p (k t) -> p k t", k=4)
        yTsB = [Buf(f"yTs{i}") for i in range(4)]
        gt1 = [arena[:, 9216 + 1024 * i: 9216 + 1024 * (i + 1)].bitcast(F32) for i in range(2)]
        gt1B = [Buf(f"gt1_{i}") for i in range(2)]
        gt2 = [arena[:, 11264 + 1024 * i: 11264 + 1024 * (i + 1)].bitcast(F32) for i in range(2)]
        gt2B = [Buf(f"gt2_{i}") for i in range(2)]
        mT = sb("mT", [128, 8 * T], BF16)
        mT3 = mT[:].rearrange("p (k t) -> p k t", k=8)
        mTB = [Buf(f"mT{i}") for i in range(8)]
        qxT = sb("qxT", [128, 8 * T], BF16)
        qxT3 = qxT[:].rearrange("p (k t) -> p k t", k=8)
        qxTB = [Buf(f"qxT{i}") for i in range(8)]
        pxT = [sb(f"pxT{i}", [128, 2 * T], BF16) for i in range(2)]
        pxT3 = [t[:].rearrange("p (m t) -> p m t", m=2) for t in pxT]
        pxTB = [[Buf(f"pxT{i}_{m}") for m in range(2)] for i in range(2)]
        rden, rdenB = gv, gvB
        kxT = sb("kxT", [128, 8 * 256], BF16)
        kxT3 = kxT[:].rearrange("p (k m) -> p k m", k=8)
        kxTB = Buf("kxT")
        vx = sb("vx", [128, 2 * D], BF16)
        vx3 = vx[:].rearrange("p (m c) -> p m c", m=2)
        vxB = Buf("vx")
        def actc(c):
            return arena[:, c * 512:(c + 1) * 512]
        actTB = [Buf(f"actT{i}") for i in range(22)]
        sil, silB = gv, gvB
        ost = [yTa[:].bitcast(F32), yTs[:].bitcast(F32)]
        ostB = [Buf(f"ost{i}") for i in range(2)]
        alias([ostB[0]], yTaB)
        alias([ostB[1]], yTsB)
        alias(actTB, [b for l in pTB for b in l] + guB + gt1B + gt2B)
        E3 = sb("E3", [128, 1024], F32)
        E4 = sb("E4", [128, 1024], F32)
        ctmp = qxT[:, 0:2048].bitcast(F32)
        ctmp2 = qxT[:, 2048:4096].bitcast(F32)
        gfin = sb("gfin", [128, D], F32)
        gTs = sb("gTs", [128, 32], F32)
        lgB = sb("lgB", [128, 512], F32)
        lbB = junk[:].bitcast(F32)
        Ctab = sb("Ctab", [128, 512], F32)
        WsT = sb("WsT", [128, 1024], BF16)
        wsum = sb("wsum", [128, 8], F32)
        sgbT = sb("sgbT", [128, 8], F32)
        hfl = sb("hfl", [128, 8], F32)
        epsc = sb("epsc", [128, 8], F32)
        ident = sb("ident", [128, 128], BF16)
        ones = sb("ones", [128, 128], BF16)
        constB = Buf("const")
        identB = Buf("ident")
        E3B, E4B, ctmpB, ctmp2B, CtabB, WsTB, wsumB, onesB = (Buf(n) for n in
                                                             ("E3", "E4", "ctmp", "ctmp2", "Ctab", "WsT", "wsum", "ones"))
        alias([ctmpB, ctmp2B], qxTB)

        PS = [es.enter_context(nc.psum_tensor(f"ps{i}", [128, 512], F32)) for i in range(8)]
        PSB = [Buf(f"ps{i}") for i in range(8)]
        rr = [0]

        def nb():
            i = rr[0] % 8
            rr[0] += 1
            return PS[i], PSB[i]

        def v3(ap2, k):
            return ap2.rearrange("p (k c) -> p k c", k=k)

        def wview(w):
            return w.rearrange("(k p) c -> p k c", p=128)

        units = {}
        uid = [0]

        def defunit(name, parts):
            units[name] = (uid[0], parts)
            uid[0] += 1

        wi = wview(w_in)
        for i, nm in enumerate(["q", "k", "v", "u", "vs", "ga0", "ga1", "gb0", "gb1"]):
            defunit(nm, [(lambda s: v3(s, 8), wi[:, :, i * 512:(i + 1) * 512])])
        defunit("wa", [(lambda s: v3(s, 4), wview(w_ba))])
        defunit("wb", [(lambda s: v3(s, 4), wview(w_bs))])
        for nm, w in (("wo", w_out), ("xq", w_xq), ("xo", w_xo)):
            for h in range(2):
                defunit(f"{nm}{h}", [(lambda s: v3(s, 8), wview(w)[:, :, h * 512:(h + 1) * 512])])
        wf = wview(w_fi)
        for j in range(11):
            defunit(f"fi{j}", [
                (lambda s: v3(s, 8)[:, :, 0:256], wf[:, :, 256 * j:256 * j + 256]),
                (lambda s: v3(s, 8)[:, :, 256:512], wf[:, :, DFF + 256 * j:DFF + 256 * j + 256]),
            ])
        wfo = wview(w_fo)
        for j in range(6):
            nk = 4 if j < 5 else 2
            defunit(f"fo{j}", [(lambda s, nk=nk: v3(s, 4)[:, 0:nk, :], wfo[:, 4 * j:4 * j + nk, :])])
        wkv = wview(w_xkv)
        for i in range(4):
            defunit(f"xkv{i}", [(lambda s: v3(s, 8), wkv[:, :, i * 512:(i + 1) * 512])])
        assert uid[0] <= NUNIT
        scrB = {}
        in_scr = set()

        pass_units = (["u", "vs", "q", "k", "v", "wa", "wb", "ga0", "gb0", "ga1", "gb1", "wo0", "wo1",
                       "xq0", "xq1", "xo0", "xo1"] + [f"fi{j}" for j in range(11)] + [f"fo{j}" for j in range(6)])
        seq = ["xkv0", "xkv1", "xkv2", "xkv3", "k", "v"]
        for _ in range(NPASS):
            seq += pass_units
        st = {"loaded": 0, "done": [False] * len(seq), "pos": 0}

        def emit_load(i):
            name = seq[i]
            u, parts = units[name]
            s = i % NSLOT
            slot = ring[s][:]
            if name in in_scr:
                S.dma("sp", ringB[s], [scrB[name]], [ringB[s]],
                      lambda e: e.dma_start(out=slot, in_=wscr[u]))
            else:
                S.dma("pool", ringB[s], [], [ringB[s]],
                      lambda e: [e.dma_start(out=f(slot), in_=src) for f, src in parts])
                if not name.startswith("xkv"):
                    scrB[name] = Buf("scr" + name)
                    S.dma("sp", ringB[s], [ringB[s]], [scrB[name]],
                          lambda e: e.dma_start(out=wscr[u], in_=slot))
                    in_scr.add(name)

        def pump():
            while st["loaded"] < len(seq):
                i = st["loaded"]
                if i >= NSLOT and not st["done"][i - NSLOT]:
                    break
                emit_load(i)
                st["loaded"] += 1

        def acquire(name):
            i = st["pos"]
            while seq[i] != name or st["done"][i]:
                i += 1
            pump()
            assert i < st["loaded"], (name, i, st["loaded"])
            s = i % NSLOT
            return i, ring[s][:], ringB[s]

        def release(i):
            st["done"][i] = True
            while st["pos"] < len(seq) and st["done"][st["pos"]]:
                st["pos"] += 1
            pump()

        def ck(name):
            MARKS.append((name, S.eng["pe"].n))
            if STOP == name:
                raise _Stop()

        def body():
            pump()
            for tt in range(2):
                S.dma("sp", xsB_sets[0][tt], [], [xsB_sets[0][tt]],
                      lambda e, tt=tt: e.dma_start(out=xs_sets[0][tt][:], in_=memc[tt * 128:(tt + 1) * 128, :]))
            def cload(dst, src, eng="sp"):
                S.dma(eng, constB, [], [], lambda e: e.dma_start(out=dst, in_=src))

            cload(ctmp[:], b3_d)
            cload(ctmp2[:], bfar_d)
            cload(E4[:], b4_d)
            cload(E3[:], mask4_d)
            cload(gfin[:], gfin_d.partition_broadcast(128))
            cload(gTs[:], gT_d)
            cload(lgB[:], lng_d.partition_broadcast(128))
            S.dma("sp", junkB, [], [junkB], lambda e: e.dma_start(out=lbB, in_=lnb_d.partition_broadcast(128)))
            cload(sgbT[:], sgbT_d)
            cload(hfl[:], hflag)
            cload(ident[:], ident_d, "pool")
            for tt in range(4):
                S.dma("sp", xsB_sets[1][tt], [], [xsB_sets[1][tt]],
                      lambda e, tt=tt: e.dma_start(out=xs_sets[1][tt][:], in_=xc[tt * 128:(tt + 1) * 128, :]))
            for tt in (2, 3):
                S.dma("sp", xsB_sets[0][tt], [], [xsB_sets[0][tt]],
                      lambda e, tt=tt: e.dma_start(out=xs_sets[0][tt][:], in_=xc[HALO + tt * 128:HALO + (tt + 1) * 128, :]))
            ck("c0")
            constB.w = constB.d["sp"][2], constB.d["sp"][0], constB.d["sp"][1]
            identB.w = constB.d["pool"][2], constB.d["pool"][0], constB.d["pool"][1]
            S.op("pool", [], [onesB], lambda e: e.memset(ones[:], 1.0))
            S.op("pool", [], [onesB], lambda e: e.memset(epsc[:], EPS))
            S.op("dve", [constB], [E4B], lambda e: e.tensor_tensor(out=E4[:], in0=E4[:], in1=ctmp2[:], op=ALU.subtract))
            S.op("act", [E4B], [E4B], lambda e: e.activation(out=E4[:], in_=E4[:], func=AF.Exp))
            S.op("dve", [E4B, constB], [E4B], lambda e: e.tensor_tensor(out=E4[:], in0=E4[:], in1=E3[:], op=ALU.mult))
            S.op("dve", [constB, E4B], [E3B], lambda e: e.tensor_tensor(out=E3[:], in0=ctmp[:], in1=ctmp2[:], op=ALU.subtract))
            S.op("act", [E3B], [E3B], lambda e: e.activation(out=E3[:], in_=E3[:], func=AF.Exp))
            ck("c1")
            S.dma("sp", ctmpB, [E3B], [ctmpB], lambda e: e.dma_start(out=ctmp[:], in_=sgwT_d))
            S.dma("sp", ctmp2B, [E3B], [ctmp2B], lambda e: e.dma_start(out=ctmp2[:], in_=sgmT_d))
            S.op("dve", [ctmpB, ctmp2B], [WsTB], lambda e: e.tensor_tensor(out=WsT[:], in0=ctmp[:], in1=ctmp2[:], op=ALU.mult))
            S.dma("sp", ctmpB, [], [ctmpB], lambda e: e.dma_start(out=ctmp[:], in_=sgwN_d))
            S.dma("sp", ctmp2B, [], [ctmp2B], lambda e: e.dma_start(out=ctmp2[:], in_=sgmN_d))
            S.op("dve", [ctmpB, ctmp2B], [ctmpB], lambda e: e.tensor_tensor(out=ctmp[:], in0=ctmp[:], in1=ctmp2[:], op=ALU.mult))
            S.op("dve", [ctmpB], [wsumB], lambda e: e.tensor_reduce(
                out=wsum[:], in_=ctmp[:].rearrange("p (g s) -> p g s", g=8), axis=AX.X, op=ALU.add))
            S.op("dve", [wsumB, constB, junkB], [CtabB], lambda e: e.tensor_tensor(
                out=Ctab[:].rearrange("p (g d) -> p g d", g=8), in0=lbB.rearrange("p (g d) -> p g d", g=8),
                in1=wsum[:].unsqueeze(2).to_broadcast([128, 8, 64]), op=ALU.mult))
            S.op("dve", [CtabB, constB], [CtabB], lambda e: e.tensor_tensor(
                out=Ctab[:].rearrange("p (g d) -> p g d", g=8), in0=Ctab[:].rearrange("p (g d) -> p g d", g=8),
                in1=sgbT[:].unsqueeze(2).to_broadcast([128, 8, 64]), op=ALU.add))
            ck("c2")
            for s_ in range(8):
                col = vA3[:, s_, :].rearrange("p (h e) -> p h e", e=66)[:, :, 64:65]
                if s_ < 4:
                    S.op("dve", [constB], [vAB[s_]], lambda e, col=col: e.tensor_copy(
                        out=col, in_=hfl[:].unsqueeze(2)))
                else:
                    S.op("pool", [], [vAB[s_]], lambda e, col=col: e.memset(col, 1.0))

            ck("const")
            def load_x(tt, row0):
                S.dma("sp", xsB[tt], [], [xsB[tt]],
                      lambda e: e.dma_start(out=xs[tt][:], in_=xc[row0 + tt * 128: row0 + tt * 128 + 128, :]))

            def rms_stats(tt, c=None):
                c = tt if c is None else c
                S.op("act", [xsB[tt]], [junkB, ssB[c]], lambda e: e.activation(
                    out=junk[:], in_=xs[tt][:], func=AF.Square, accum_out=st_ss[:, c:c + 1]))
                S.op("act", [ssB[c], onesB], [sdB[c]], lambda e: e.activation(
                    out=st_sd[:, c:c + 1], in_=st_ss[:, c:c + 1], func=AF.Ln, scale=1.0 / D, bias=epsc[:, 0:1]))
                S.op("act", [sdB[c]], [rsB[c]], lambda e: e.activation(
                    out=st_rs[:, c:c + 1], in_=st_sd[:, c:c + 1], func=AF.Exp, scale=-0.5))

            def norm_pre(tt):
                rms_stats(tt)
                hb = tt % 2
                S.op("act", [xsB[tt], rsB[tt]], [hnB[hb]], lambda e: e.activation(
                    out=hn[hb][:], in_=xs[tt][:], func=AF.Copy, scale=st_rs[:, tt:tt + 1]))

            def norm_tr(tt, gidx, bi=None):
                hb = tt % 2
                bank, bB = nb() if bi is None else (PS[bi], PSB[bi])
                bv = bank[:].bitcast(BF16)

                def tr(e):
                    for k in range(8):
                        ins = e.transpose(out=bv[:, k * 128:(k + 1) * 128], in_=hn[hb][:, k * 128:(k + 1) * 128],
                                          identity=ident[:])
                    return ins
                S.op("pe", [hnB[hb], identB], [bB], tr)
                S.op("dve", [bB, constB], [hTB[tt]], lambda e: e.tensor_tensor(
                    out=hT3[:, :, tt * 128:(tt + 1) * 128], in0=bv.rearrange("p (k t) -> p k t", k=8),
                    in1=gTs[:, gidx * 8:(gidx + 1) * 8].unsqueeze(2).to_broadcast([128, 8, 128]), op=ALU.mult))

            def norm_T(tts, gidx):
                for tt in tts:
                    norm_pre(tt)
                    norm_tr(tt, gidx)

            def proj_fm(slot, sB, c0, src3, srcBs, nk, ntok, kslice=None):
                bank, bB = nb()

                def f(e):
                    for k in range(nk):
                        ins = e.matmul(bank[:, 0:ntok], lhsT=slot[:, k, c0:c0 + 128], rhs=src3[:, k, 0:ntok],
                                       start=(k == 0), stop=(k == nk - 1))
                    return ins
                S.op("pe", [sB] + srcBs, [bB], f)
                return bank, bB

            def proj_fm_split(slot, sB, c0, src3, nk):
                bank, bB = nb()

                def half(hf):
                    def f(e):
                        for k in range(nk):
                            ins = e.matmul(bank[:, hf * 256:(hf + 1) * 256], lhsT=slot[:, k, c0:c0 + 128],
                                           rhs=src3[:, k, hf * 256:(hf + 1) * 256], start=(k == 0), stop=(k == nk - 1))
                        return ins
                    S.op("pe", [sB, hTB[2 * hf], hTB[2 * hf + 1]], [bB], f)
                half(0)
                return bank, bB, (lambda: half(1))

            def proj_tm(slot, sB, c0, src3, srcB, nk, tt, bank=None, bB=None, first=True, last=True, k0=0):
                if bank is None:
                    bank, bB = nb()

                def f(e):
                    for k in range(nk):
                        ins = e.matmul(bank[:], lhsT=src3[:, k0 + k, tt * 128:(tt + 1) * 128], rhs=slot[:, k, c0:c0 + 512],
                                       start=(first and k == 0), stop=(last and k == nk - 1))
                    return ins
                S.op("pe", [sB] + srcB, [bB], f)
                return bank, bB

            norm_T([0, 1], 2)
            for tt in (0, 1):
                load_x(tt, HALO)
            use_set(1)
            norm_pre(0)
            norm_pre(1)
            use_set(0)
            for i in range(2):
                ui, slot, sB = acquire(f"xkv{i}")
                s3 = v3(slot, 8)
                for ff in range(4):
                    f_ = i * 4 + ff
                    bank, bB = proj_fm(s3, sB, ff * 128, hT3, [hTB[0], hTB[1]], 8, 256)
                    S.op("act", [bB], [kxTB], lambda e, bank=bank, f_=f_: e.activation(
                        out=kxT3[:, f_, :], in_=bank[:, 0:256], func=AF.Copy))
                release(ui)
            for i in range(2):
                ui, slot, sB = acquire(f"xkv{2 + i}")
                s3 = v3(slot, 8)
                for mt in range(2):
                    bank, bB = proj_tm(s3, sB, 0, hT3, [hTB[mt]], 8, mt)
                    S.op("act", [bB], [vxB], lambda e, bank=bank, mt=mt, i=i: e.activation(
                        out=vx3[:, mt, i * 512:(i + 1) * 512], in_=bank[:], func=AF.Copy))
                release(ui)

            ck("kvpro")
            def k_items(p):
                half = (p + 1) % 2
                ui, slot, sB = acquire("k")
                s3 = v3(slot, 8)

                def item(a):
                    bank, bB = proj_fm(s3, sB, a * 128, hT3, hTB, 8, T)
                    S.op("act", [bB], [kTB[half][a]], lambda e: e.activation(
                        out=kT3[:, a, half * T:(half + 1) * T], in_=bank[:], func=AF.Copy))
                    if a == 3:
                        release(ui)
                return item

            def v_items(p):
                ui, slot, sB = acquire("v")
                s3 = v3(slot, 8)

                def item(tt):
                    s_ = (4 * p + tt + 4) % 8
                    bank, bB = proj_tm(s3, sB, 0, hT3, [hTB[tt]], 8, tt)
                    S.op("act", [bB], [vAB[s_]], lambda e: e.activation(
                        out=vA3[:, s_, :].rearrange("p (h e) -> p h e", e=66)[:, :, 0:64],
                        in_=bank[:].rearrange("p (h d) -> p h d", d=64), func=AF.Copy))
                    if tt == 3:
                        release(ui)
                return item

            def kv_proj(p):
                ki = k_items(p)
                for a in range(4):
                    ki(a)
                vi = v_items(p)
                for tt in range(4):
                    vi(tt)

            use_set(1)
            norm_tr(0, 0)
            norm_tr(1, 0)
            norm_pre(2)
            norm_pre(3)
            norm_tr(2, 0)
            norm_tr(3, 0)
            use_set(0)
            norm_pre(0)
            norm_pre(1)
            use_set(1)
            kv_proj(-1)
            use_set(0)

            ck("halo")
            pending = []
            for p in range(NPASS):
                row0 = HALO + p * T
                half = (p + 1) % 2
                use_set(p % 2)
                if p == 0:
                    norm_tr(0, 0)
                    norm_tr(1, 0)
                    norm_pre(2)
                    norm_pre(3)
                    norm_tr(2, 0)
                    norm_tr(3, 0)
                def prefetch_next():
                    if p + 1 < NPASS:
                        use_set((p + 1) % 2)
                        for tt in range(4):
                            load_x(tt, row0 + T)
                        use_set(p % 2)

                def run_pending(_):
                    if pending:
                        pending.pop()()
                    prefetch_next()
                if not pending:
                    run_pending(None)
                ck("start")
                ui, slot, sB = acquire("u")
                s3 = v3(slot, 8)
                for tt in range(4):
                    bank, bB = proj_tm(s3, sB, 0, hT3, [hTB[tt]], 8, tt)
                    S.op("act", [bB], [guB[tt]], lambda e, bank=bank, tt=tt: e.activation(
                        out=gu[tt][:], in_=bank[:], func=AF.Gelu_apprx_tanh))
                release(ui)
                uvs, slot_vs, sB_vs = acquire("vs")
                s3vs = v3(slot_vs, 8)

                def sg1(tt):
                    b2 = tt % 2
                    bank, bB = proj_tm(s3vs, sB_vs, 0, hT3, [hTB[tt]], 8, tt)
                    S.op("act", [bB], [gvB[b2]], lambda e: e.activation(out=gv[b2][:], in_=bank[:], func=AF.Gelu_apprx_tanh))
                    g3 = gv[b2][:].rearrange("p (g d) -> p g d", g=8)
                    L = lst[:, b2 * 32:(b2 + 1) * 32]
                    S.op("dve", [gvB[b2]], [lstB[b2]], lambda e: e.tensor_reduce(
                        out=L[:, 0:8], in_=g3, axis=AX.X, op=ALU.add))
                    S.op("dve", [gvB[b2]], [sqB], lambda e: e.tensor_tensor(
                        out=sq[:], in0=gv[b2][:], in1=gv[b2][:], op=ALU.mult))
                    S.op("dve", [sqB, lstB[b2]], [lstB[b2]], lambda e: e.tensor_reduce(
                        out=L[:, 8:16], in_=sq[:].rearrange("p (g d) -> p g d", g=8), axis=AX.X, op=ALU.add))
                    S.op("dve", [lstB[b2]], [lstB[b2]], lambda e: e.tensor_scalar(
                        out=L[:, 16:24], in0=L[:, 0:8], scalar1=1.0 / 64, scalar2=None, op0=ALU.mult))
                    S.op("dve", [lstB[b2]], [lstB[b2]], lambda e: e.tensor_tensor(
                        out=L[:, 0:8], in0=L[:, 16:24], in1=L[:, 16:24], op=ALU.mult))
                    S.op("dve", [lstB[b2]], [lstB[b2]], lambda e: e.scalar_tensor_tensor(
                        out=L[:, 24:32], in0=L[:, 8:16], scalar=1.0 / 64, in1=L[:, 0:8], op0=ALU.mult, op1=ALU.subtract))
                    S.op("act", [lstB[b2], onesB], [lstB[b2]], lambda e: e.activation(
                        out=L[:, 24:32], in_=L[:, 24:32], func=AF.Ln, bias=epsc[:, 0:1]))
                    S.op("act", [lstB[b2]], [lstB[b2]], lambda e: e.activation(
                        out=L[:, 24:32], in_=L[:, 24:32], func=AF.Exp, scale=-0.5))
                    S.op("dve", [lstB[b2], gvB[b2]], [gvB[b2]], lambda e: e.tensor_tensor(
                        out=g3, in0=g3, in1=L[:, 16:24].unsqueeze(2).to_broadcast([128, 8, 64]), op=ALU.subtract))
                    S.op("dve", [lstB[b2], gvB[b2]], [vnB[b2]], lambda e: e.tensor_tensor(
                        out=vn[b2][:].rearrange("p (g d) -> p g d", g=8), in0=g3,
                        in1=L[:, 24:32].unsqueeze(2).to_broadcast([128, 8, 64]), op=ALU.mult))
                    if tt == 3:
                        release(uvs)

                ck("u")
                uq, slot_q, sB_q = acquire("q")
                s3q = v3(slot_q, 8)

                def qi(a):
                    bank, bB = proj_fm(s3q, sB_q, a * 128, hT3, hTB, 8, T)
                    S.op("act", [bB], [qTB[a]], lambda e: e.activation(
                        out=qT3[:, a, :], in_=bank[:], func=AF.Copy, scale=0.125))
                    if a == 3:
                        release(uq)
                ki = k_items(p)
                vi = v_items(p)

                def sg2(tt):
                    b2 = tt % 2
                    bank2, bB2 = nb()

                    def sgmm(e):
                        for g in range(8):
                            ins = e.matmul(bank2[:, g * 64:(g + 1) * 64], lhsT=WsT[:, g * 128:(g + 1) * 128],
                                           rhs=vn[b2][:, g * 64:(g + 1) * 64], start=True, stop=True)
                        return ins
                    S.op("pe", [vnB[b2], WsTB], [bB2], sgmm)
                    S.op("dve", [bB2, constB], [svtB[b2]], lambda e: e.tensor_tensor(
                        out=svt[b2][:], in0=bank2[:], in1=lgB[:], op=ALU.mult))
                    S.op("dve", [svtB[b2], CtabB], [svtB[b2]], lambda e: e.tensor_tensor(
                        out=svt[b2][:], in0=svt[b2][:], in1=Ctab[:], op=ALU.add))
                    S.op("dve", [svtB[b2], guB[tt]], [ysgB[b2]], lambda e: e.tensor_tensor(
                        out=ysg[b2][:], in0=svt[b2][:], in1=gu[tt][:], op=ALU.mult))

                def sg3(tt):
                    b2 = tt % 2
                    bank3, bB3 = nb()
                    bv3 = bank3[:].bitcast(BF16)

                    def tr2(e):
                        for k in range(4):
                            ins = e.transpose(out=bv3[:, k * 128:(k + 1) * 128], in_=ysg[b2][:, k * 128:(k + 1) * 128],
                                              identity=ident[:])
                        return ins
                    S.op("pe", [ysgB[b2], identB], [bB3], tr2)
                    S.op("dve", [bB3], [yTsB[tt]], lambda e: e.tensor_copy(
                        out=yTs3[:, :, tt * 128:(tt + 1) * 128], in_=bv3[:, 0:512].rearrange("p (k t) -> p k t", k=4)))

                def attA(i):
                    qp, hg, pb = i // 2, i % 2, i % 2
                    gtile = 4 * p + qp
                    slots = [(gtile + j) % 8 for j in range(5)]
                    for j in range(5):
                        s_ = slots[j]
                        kh = s_ // 4
                        bank, bB = nb()

                        def sc(e, bank=bank, s_=s_):
                            for hh in range(4):
                                a, hp_ = hh, hg
                                ins = e.matmul(bank[:, hh * 128:(hh + 1) * 128],
                                               lhsT=kT3[hp_ * 64:(hp_ + 1) * 64, a, s_ * 128:(s_ + 1) * 128],
                                               rhs=qT3[hp_ * 64:(hp_ + 1) * 64, a, qp * 128:(qp + 1) * 128],
                                               start=True, stop=True)
                            return ins
                        S.op("pe", qTB + kTB[kh], [bB], sc)
                        S.op("act", [bB], [pTB[pb][j]], lambda e, bank=bank, j=j: e.activation(
                            out=pT3[pb][:, j, :], in_=bank[:], func=AF.Exp))
                        if j == 0:
                            S.op("pool", [], [pTB[pb][0]], lambda e: e.memset(
                                pT[pb][0:64, 0:512].rearrange("p (h i) -> p h i", i=128)[:, :, 64:128], 0.0))
                        elif j >= 3:
                            Et, EtB = (E3, E3B) if j == 3 else (E4, E4B)
                            S.op("dve", [EtB, pTB[pb][j]], [pTB[pb][j]], lambda e, j=j, Et=Et: e.tensor_tensor(
                                out=pT3[pb][:, j, :].rearrange("p (hh i) -> p hh i", i=128),
                                in0=pT3[pb][:, j, :].rearrange("p (hh i) -> p hh i", i=128),
                                in1=Et[:].rearrange("p (hh two i) -> p two hh i", two=2, i=128)[:, hg], op=ALU.mult))

                def attB(i):
                    qp, hg, pb = i // 2, i % 2, i % 2
                    yb = qp % 2
                    gtile = 4 * p + qp
                    slots = [(gtile + j) % 8 for j in range(5)]
                    bankO, bOB = nb()

                    def pv(e):
                        for hh in range(4):
                            h = 2 * hh + hg
                            for j in range(5):
                                ins = e.matmul(bankO[:, hh * 65:(hh + 1) * 65],
                                               lhsT=pT3[pb][:, j, hh * 128:(hh + 1) * 128],
                                               rhs=vA3[:, slots[j], h * 66:h * 66 + 65],
                                               start=(j == 0), stop=(j == 4), skip_group_check=True)
                        return ins
                    S.op("pe", pTB[pb] + [vAB[s_] for s_ in slots], [bOB], pv)
                    o3 = bankO[:, 0:260].rearrange("p (h e) -> p h e", e=65)
                    rcv = rc[:, pb * 4:(pb + 1) * 4]
                    S.op("dve", [bOB], [rcB[pb]], lambda e: e.reciprocal(out=rcv.unsqueeze(2), in_=o3[:, :, 64:65]))
                    S.op("dve", [bOB, rcB[pb]], [yattB[yb]], lambda e: e.tensor_tensor(
                        out=yatt[yb][:].rearrange("p (hh two d) -> p two hh d", two=2, d=64)[:, hg],
                        in0=o3[:, :, 0:64], in1=rcv.unsqueeze(2).to_broadcast([128, 4, 64]), op=ALU.mult))
                    if hg == 1:
                        bank3, bB3 = nb()
                        bv3 = bank3[:].bitcast(BF16)

                        def tr3(e):
                            for k in range(4):
                                ins = e.transpose(out=bv3[:, k * 128:(k + 1) * 128], in_=yatt[yb][:, k * 128:(k + 1) * 128],
                                                  identity=ident[:])
                            return ins
                        S.op("pe", [yattB[yb], identB], [bB3], tr3)
                        S.op("dve", [bB3], [yTaB[qp]], lambda e: e.tensor_copy(
                            out=yTa3[:, :, qp * 128:(qp + 1) * 128], in_=bv3[:, 0:512].rearrange("p (k t) -> p k t", k=4)))

                order = [(sg1, 0), (sg1, 1), (qi, 0), (qi, 1), (qi, 2), (qi, 3), (run_pending, None) if pending else (lambda _: None, None), (sg2, 0), (ki, 0), (ki, 1), (sg2, 1),
                         (ki, 2), (ki, 3), (sg3, 0), (sg3, 1), (sg1, 2), (sg1, 3), (vi, 0), (vi, 1), (vi, 2), (vi, 3),
                         (sg2, 2), (attA, 0), (sg2, 3), (attA, 1), (sg3, 2), (attB, 0), (sg3, 3), (attA, 2), (attB, 1),
                         (attA, 3), (attB, 2), (attA, 4), (attB, 3), (attA, 5), (attB, 4), (attA, 6), (attB, 5), (attA, 7),
                         (attB, 6), (attB, 7)]
                for fn_, arg_ in order:
                    fn_(arg_)
                ck("sg")
                if p == 0:
                    for s_ in range(4):
                        col = vA3[:, s_, :].rearrange("p (h e) -> p h e", e=66)[:, :, 64:65]
                        S.op("pool", [], [vAB[s_]], lambda e, col=col: e.memset(col, 1.0))
                ck("attn")
                ua, slA, sAB = acquire("wa")
                ub, slB, sBB = acquire("wb")
                a3, b3_ = v3(slA, 4), v3(slB, 4)
                gsl = {}

                def gacq(fh):
                    uga, slGA, sGAB = acquire(f"ga{fh}")
                    ugb, slGB, sGBB = acquire(f"gb{fh}")
                    gsl[fh] = (uga, v3(slGA, 8), sGAB, ugb, v3(slGB, 8), sGBB)

                def gG(f_):
                    fh, ff, tb = f_ // 4, f_ % 4, f_ % 2
                    if fh not in gsl:
                        gacq(fh)
                    uga, ga3, sGAB, ugb, gb3, sGBB = gsl[fh]
                    bga, bgaB = proj_fm(ga3, sGAB, ff * 128, hT3, hTB, 8, T)
                    S.op("act", [bgaB], [gt1B[tb]], lambda e: e.activation(out=gt1[tb][:], in_=bga[:], func=AF.Sigmoid))
                    bgb, bgbB = proj_fm(gb3, sGBB, ff * 128, hT3, hTB, 8, T)
                    S.op("act", [bgbB], [gt2B[tb]], lambda e: e.activation(out=gt2[tb][:], in_=bgb[:], func=AF.Sigmoid))
                    if ff == 3:
                        release(uga)
                        release(ugb)

                def gP(f_):
                    tb = f_ % 2
                    bpa, bpaB = proj_fm(a3, sAB, f_ * 128, yTa3, yTaB, 4, T)
                    S.op("dve", [gt1B[tb], bpaB], [gt1B[tb]], lambda e: e.tensor_tensor(
                        out=gt1[tb][:], in0=bpa[:], in1=gt1[tb][:], op=ALU.mult))
                    bpb, bpbB = proj_fm(b3_, sBB, f_ * 128, yTs3, yTsB, 4, T)
                    S.op("dve", [gt2B[tb], bpbB], [gt2B[tb]], lambda e: e.tensor_tensor(
                        out=gt2[tb][:], in0=bpb[:], in1=gt2[tb][:], op=ALU.mult))
                    S.op("dve", [gt1B[tb], gt2B[tb]], [mTB[f_]], lambda e: e.tensor_tensor(
                        out=mT3[:, f_, :], in0=gt1[tb][:], in1=gt2[tb][:], op=ALU.add))

                gG(0); gG(1)
                for f_ in range(8):
                    gP(f_)
                    if f_ + 2 < 8:
                        gG(f_ + 2)
                release(ua)
                release(ub)

                def resid_proj(uname, src3, srcBs, gidx, filler=None):
                    u0, sl0, sB0 = acquire(f"{uname}0")
                    u1, sl1, sB1 = acquire(f"{uname}1")
                    sl = [(v3(sl0, 8), sB0), (v3(sl1, 8), sB1)]

                    def mm(tt):
                        for hc in range(2):
                            bank, bB = proj_tm(sl[hc][0], sl[hc][1], 0, src3, srcBs, 8, tt)
                            S.op("dve", [bB, xsB[tt]], [xsB[tt]], lambda e, bank=bank, tt=tt, hc=hc: e.tensor_tensor(
                                out=xs[tt][:, hc * 512:(hc + 1) * 512], in0=bank[:], in1=xs[tt][:, hc * 512:(hc + 1) * 512],
                                op=ALU.add))
                    mm(0); norm_pre(0)
                    mm(1); norm_pre(1)
                    mm(2); norm_tr(0, gidx); norm_pre(2)
                    mm(3); norm_tr(1, gidx); norm_pre(3)
                    release(u0)
                    release(u1)
                    cont = filler() if filler is not None else None
                    norm_tr(2, gidx)
                    norm_tr(3, gidx)
                    if cont is not None:
                        cont()

                ck("gate")

                def xq_filler():
                    ui, slot, sB = acquire("xq0")
                    s3 = v3(slot, 8)
                    items = [proj_fm_split(s3, sB, ff * 128, hT3, 8) for ff in range(4)]

                    def cont():
                        for ff, (bank, bB, c2) in enumerate(items):
                            c2()
                            S.op("act", [bB], [qxTB[ff]], lambda e, bank=bank, ff=ff: e.activation(
                                out=qxT3[:, ff, :], in_=bank[:], func=AF.Copy, scale=1.0 / 16))
                        release(ui)
                    return cont
                resid_proj("wo", mT3, mTB, 1, xq_filler)
                ck("wo")
                for hc in range(1, 2):
                    ui, slot, sB = acquire(f"xq{hc}")
                    s3 = v3(slot, 8)
                    for ff in range(4):
                        f_ = hc * 4 + ff
                        bank, bB = proj_fm(s3, sB, ff * 128, hT3, hTB, 8, T)
                        S.op("act", [bB], [qxTB[f_]], lambda e, bank=bank, f_=f_: e.activation(
                            out=qxT3[:, f_, :], in_=bank[:], func=AF.Copy, scale=1.0 / 16))
                    release(ui)
                ck("xq")
                def xsc(h):
                    xb = h % 2
                    for mt in range(2):
                        bank, bB = nb()

                        def xs_(e, bank=bank, mt=mt):
                            for ee in range(2):
                                ins = e.matmul(bank[:], lhsT=kxT3[:, 2 * h + ee, mt * 128:(mt + 1) * 128],
                                               rhs=qxT3[:, 2 * h + ee, :], start=(ee == 0), stop=(ee == 1))
                            return ins
                        S.op("pe", [kxTB, qxTB[2 * h], qxTB[2 * h + 1]], [bB], xs_)
                        S.op("act", [bB], [pxTB[xb][mt]], lambda e, bank=bank, mt=mt: e.activation(
                            out=pxT3[xb][:, mt, :], in_=bank[:], func=AF.Exp))

                def xpv_(h):
                    xb = h % 2
                    bankD, bDB = nb()

                    def den(e):
                        for mt in range(2):
                            ins = e.matmul(bankD[:], lhsT=ones[:], rhs=pxT3[xb][:, mt, :], start=(mt == 0), stop=(mt == 1))
                        return ins
                    S.op("pe", pxTB[xb] + [onesB], [bDB], den)
                    S.op("act", [bDB], [rdenB[xb]], lambda e: e.activation(out=rden[xb][:], in_=bankD[:], func=AF.Ln))
                    S.op("act", [rdenB[xb]], [rdenB[xb]], lambda e: e.activation(
                        out=rden[xb][:], in_=rden[xb][:], func=AF.Exp, scale=-1.0))
                    for ee in range(2):
                        bank, bB = nb()

                        def xpv(e, bank=bank, ee=ee):
                            for mt in range(2):
                                ins = e.matmul(bank[:], lhsT=vx3[:, mt, (2 * h + ee) * 128:(2 * h + ee + 1) * 128],
                                               rhs=pxT3[xb][:, mt, :], start=(mt == 0), stop=(mt == 1))
                            return ins
                        S.op("pe", pxTB[xb] + [vxB], [bB], xpv)
                        S.op("dve", [bB, rdenB[xb]], [mTB[2 * h + ee]], lambda e, bank=bank, ee=ee: e.tensor_tensor(
                            out=mT3[:, 2 * h + ee, :], in0=bank[:], in1=rden[xb][:], op=ALU.mult))

                xsc(0); xsc(1); xpv_(0); xsc(2); xpv_(1); xsc(3); xpv_(2); xpv_(3)
                ck("xcore")
                def ffn_item_evac(c, bg, bgB, bu, buB):
                    sbuf_ = c % 2
                    S.op("act", [bgB], [silB[sbuf_]], lambda e: e.activation(out=sil[sbuf_][:], in_=bg[:], func=AF.Silu))
                    dst = actc(c)
                    S.op("dve", [silB[sbuf_], buB], [actTB[c]], lambda e: e.tensor_tensor(
                        out=dst, in0=bu[:], in1=sil[sbuf_][:], op=ALU.mult))

                def ffn_filler():
                    ui, slot, sB = acquire("fi0")
                    s3 = v3(slot, 8)
                    items = []
                    for ee in range(2):
                        items.append((ee, proj_fm_split(s3, sB, ee * 128, hT3, 8), proj_fm_split(s3, sB, 256 + ee * 128, hT3, 8)))

                    def cont():
                        for ee, (bg, bgB, cg), (bu, buB, cu) in items:
                            cg()
                            cu()
                            ffn_item_evac(ee, bg, bgB, bu, buB)
                        release(ui)
                    return cont
                resid_proj("xo", mT3, mTB, 3, ffn_filler)
                ck("xattn")
                look = p + 1 < NPASS

                def la(fn_, *a_):
                    use_set((p + 1) % 2)
                    fn_(*a_)
                    use_set(p % 2)
                for j in range(1, 11):
                    ui, slot, sB = acquire(f"fi{j}")
                    s3 = v3(slot, 8)
                    for ee in range(2):
                        c = 2 * j + ee
                        sbuf_ = c % 2
                        bg, bgB = proj_fm(s3, sB, ee * 128, hT3, hTB, 8, T)
                        bu, buB = proj_fm(s3, sB, 256 + ee * 128, hT3, hTB, 8, T)
                        S.op("act", [bgB], [silB[sbuf_]], lambda e, bg=bg, sbuf_=sbuf_: e.activation(
                            out=sil[sbuf_][:], in_=bg[:], func=AF.Silu))
                        dst = actc(c)
                        S.op("dve", [silB[sbuf_], buB], [actTB[c]], lambda e, bu=bu, sbuf_=sbuf_, dst=dst: e.tensor_tensor(
                            out=dst, in0=bu[:], in1=sil[sbuf_][:], op=ALU.mult))
                    release(ui)
                    if look and j == 5:
                        la(norm_pre, 0)
                        la(norm_pre, 1)
                if look:
                    la(norm_tr, 0, 0)
                    la(norm_tr, 1, 0)
                    la(norm_pre, 2)
                    la(norm_pre, 3)
                ck("ffnin")
                for j in range(6):
                    nk = 4 if j < 5 else 2
                    ui, slot, sB = acquire(f"fo{j}")
                    s3 = v3(slot, 4)
                    if j == 0:
                        bord = [(rr[0] + i_) % 8 for i_ in range(8)]
                    for bi in bord:
                        tt, hc = bi // 2, bi % 2
                        if True:

                            def fo(e, bi=bi, j=j, nk=nk, tt=tt, hc=hc, s3=s3):
                                for k in range(nk):
                                    c = 4 * j + k
                                    src = actc(c)
                                    ins = e.matmul(PS[bi][:], lhsT=src[:, tt * 128:(tt + 1) * 128],
                                                   rhs=s3[:, k, hc * 512:(hc + 1) * 512],
                                                   start=(j == 0 and k == 0), stop=(j == 5 and k == nk - 1))
                                return ins
                            S.op("pe", [sB] + actTB[4 * j:4 * j + nk], [PSB[bi]], fo)
                    release(ui)
                for tt in range(4):
                    for hc in range(2):
                        bi = tt * 2 + hc
                        S.op("dve", [PSB[bi], xsB[tt]], [xsB[tt]], lambda e, bi=bi, tt=tt, hc=hc: e.tensor_tensor(
                            out=xs[tt][:, hc * 512:(hc + 1) * 512], in0=PS[bi][:], in1=xs[tt][:, hc * 512:(hc + 1) * 512],
                            op=ALU.add))
                    if look and tt == 1:
                        la(norm_tr, 2, 0, 0)
                        la(norm_tr, 3, 0, 1)
                        rr[0] = 2
                ck("ffn")
                def final_norm(pp=p):
                    use_set(pp % 2)
                    for tt in range(4):
                        rms_stats(tt, 4 + tt)
                        ob = tt % 2
                        S.op("dve", [xsB[tt], rsB[4 + tt], constB], [ostB[ob]], lambda e, tt=tt, ob=ob: e.scalar_tensor_tensor(
                            out=ost[ob][:], in0=xs[tt][:], scalar=st_rs[:, 4 + tt:5 + tt], in1=gfin[:], op0=ALU.mult, op1=ALU.mult))
                        r0 = pp * T + tt * 128
                        S.dma("sp", ostB[ob], [ostB[ob]], [], lambda e, ob=ob, r0=r0: e.dma_start(
                            out=out[r0:r0 + 128, :], in_=ost[ob][:]))
                    use_set((pp + 1) % 2 if pp + 1 < NPASS else pp % 2)
                if p + 1 < NPASS:
                    pending.append(final_norm)
                else:
                    final_norm()
                ck(f"pass{p}")
        def drain():
            bl = list(ostB)
            if STOP is not None:
                bl += ringB + xsB + [constB, ctmpB, ctmp2B]
            for b in bl:
                for d in b.d.values():
                    nc.sync.wait_ge(d[0], d[1])
            if STOP is not None:
                for k in S.eng:
                    if S.cnt[k] > 0:
                        nc.sync.wait_ge(S.sem[k], S.cnt[k])
        try:
            body()
        except _Stop:
            pass
        drain()
    return nc


_CACHE = {}


def _tables(rel_bias, sg_w, sg_b):
    rb = rel_bias[0]
    m = np.arange(128)[:, None]
    i = np.arange(128)[None, :]
    tabs = []
    for j in (3, 4):
        dist = (4 - j) * 128 + i - m
        idx = np.clip(dist, -128, 128) + 128
        tabs.append(np.ascontiguousarray(rb[:, idx].transpose(1, 0, 2)).reshape(128, 1024))
    bfar = np.ascontiguousarray(np.broadcast_to(rb[:, 256][None, :, None], (128, 8, 128))).reshape(128, 1024)
    mask4 = np.where((i // 64 == 0) & (m // 64 == 1), 0.0, 1.0).astype(np.float32)
    mask4 = np.ascontiguousarray(np.broadcast_to(mask4[:, None, :], (128, 8, 128))).reshape(128, 1024)
    w = sg_w[0]
    sgwT = np.ascontiguousarray(w.transpose(2, 0, 1)).reshape(128, 1024)
    sgwN = np.ascontiguousarray(w.transpose(1, 0, 2)).reshape(128, 1024)
    t_ = np.arange(128)
    mk = ((t_[None, :] // 64) <= (t_[:, None] // 64)).astype(np.float32)
    sgmN = np.ascontiguousarray(np.broadcast_to(mk[:, None, :], (128, 8, 128))).reshape(128, 1024)
    sgmT = np.ascontiguousarray(np.broadcast_to(mk.T[:, None, :], (128, 8, 128))).reshape(128, 1024)
    sgbT = np.ascontiguousarray(sg_b[0].T)
    return tabs[0], tabs[1], bfar, mask4, sgwT, sgmT, sgwN, sgmN, sgbT


def kernel(x, mem, norm_mix_g, w_in, rel_bias, sg_ln_g, sg_ln_b, sg_w, sg_b,
           w_branch_att, w_branch_sg, w_out, norm_xattn_g, norm_mem_g,
           w_xq, w_xkv, w_xo, norm_ffn_g, w_ffn_in, w_ffn_out, norm_final_g):
    f = lambda a: np.ascontiguousarray(np.asarray(a), dtype=np.float32)
    x = f(x); mem = f(mem)
    if "nc" not in _CACHE:
        _CACHE["nc"] = build_program()
    nc = _CACHE["nc"]
    b3, b4, bfar, mask4, sgwT, sgmT, sgwN, sgmN, sgbT = _tables(f(rel_bias), f(sg_w), f(sg_b))
    gT = np.concatenate([f(g)[0].reshape(8, 128).T for g in (norm_mix_g, norm_xattn_g, norm_mem_g, norm_ffn_g)], axis=1)
    shared = {
        "w_in": f(w_in)[0], "w_ba": f(w_branch_att)[0], "w_bs": f(w_branch_sg)[0], "w_out": f(w_out)[0],
        "w_xq": f(w_xq)[0], "w_xkv": f(w_xkv)[0], "w_xo": f(w_xo)[0], "w_fi": f(w_ffn_in)[0], "w_fo": f(w_ffn_out)[0],
        "gT": np.ascontiguousarray(gT), "gfin": f(norm_final_g), "b3": b3, "b4": b4, "bfar": bfar, "mask4": mask4,
        "sgwT": sgwT, "sgmT": sgmT, "sgwN": sgwN, "sgmN": sgmN, "sgbT": sgbT,
        "lng": f(sg_ln_g)[0].reshape(512), "lnb": f(sg_ln_b)[0].reshape(512),
        "ident": np.eye(128, dtype=np.float32),
    }
    in_maps = []
    for c in range(NCORES):
        b, hf = c // 2, c % 2
        own = x[b, hf * TOK:(hf + 1) * TOK]
        halo = x[b, hf * TOK - HALO: hf * TOK] if hf == 1 else np.zeros((HALO, D), np.float32)
        m = dict(shared)
        m["xc"] = np.ascontiguousarray(np.concatenate([halo, own], axis=0))
        m["memc"] = mem[b]
        m["hflag"] = np.full((128, 8), float(hf), np.float32)
        in_maps.append(m)
    res = run_bass_kernel_spmd(nc, in_maps, core_ids=list(range(NCORES)))
    outp = np.empty((4, 2 * TOK, D), np.float32)
    for c in range(NCORES):
        b, hf = c // 2, c % 2
        outp[b, hf * TOK:(hf + 1) * TOK] = res.results[c]["out"]
    return outp
```

```python
import contextlib
import numpy as np
import concourse.bass as bass
import concourse.mybir as mybir
from concourse.bass_utils import run_bass_kernel_spmd

F32 = mybir.dt.float32
BF16 = mybir.dt.bfloat16
AF = mybir.ActivationFunctionType
ALU = mybir.AluOpType
AX = mybir.AxisListType

NCORES = 8
D = 1024
T = 512
NPASS = 8
TOK = T * NPASS
HALO = 512
DFF = 2816
EPS = 1e-6
NSLOT = 6
SEM_EPOCH = 6000
OWN_WAIT = True
STOP = None


class _Stop(Exception):
    pass


class Buf:
    __slots__ = ("name", "w", "r", "al", "d")

    def __init__(self, name):
        self.name = name
        self.w = None
        self.r = {}
        self.al = []
        self.d = {}


def alias(ga, gb):
    for b in ga:
        for c in gb:
            if c not in b.al:
                b.al.append(c)
            if b not in c.al:
                c.al.append(b)


class _PEProxy:
    def __init__(self, pe):
        self._pe = pe
        self.n = 0

    def matmul(self, *a, **k):
        self.n += 1
        return self._pe.matmul(*a, **k)

    def transpose(self, *a, **k):
        self.n += 1
        return self._pe.transpose(*a, **k)

    def wait_ge(self, *a, **k):
        return self._pe.wait_ge(*a, **k)


MARKS = []


class Sched:
    def __init__(self, nc, es):
        self.nc = nc
        self.es = es
        self.eng = {"pe": _PEProxy(nc.tensor), "act": nc.scalar, "dve": nc.vector, "pool": nc.gpsimd, "sp": nc.sync}
        self.sem = {}
        self.cnt = {}
        self.key = {}
        self.known = {k: {} for k in self.eng}
        self.nsem = 0
        for k in self.eng:
            self._new_sem(k)

    def _mk(self, name):
        self.nsem += 1
        return self.es.enter_context(self.nc.semaphore(f"{name}_{self.nsem}"))

    def _new_sem(self, k):
        self.sem[k] = self._mk("e" + k)
        self.cnt[k] = 0
        self.key[k] = (k, self.nsem)

    def _deps(self, reads, writes):
        deps = {}

        def add(ev):
            if ev is None:
                return
            key, h, v = ev
            if key not in deps or deps[key][1] < v:
                deps[key] = (h, v)

        for b in reads:
            add(b.w)
        for b in writes:
            add(b.w)
            for ev in b.r.values():
                add(ev)
            for c in b.al:
                add(c.w)
                for ev in c.r.values():
                    add(ev)
        return deps

    def _wait(self, k, deps):
        e = self.eng[k]
        kn = self.known[k]
        for key, (h, v) in deps.items():
            if key == self.key[k] and (k == "pe" or not OWN_WAIT):
                continue
            if kn.get(key, 0) >= v:
                continue
            e.wait_ge(h, v)
            kn[key] = v

    def _record(self, ev, reads, writes):
        for b in writes:
            b.w = ev
            b.r = {}
        for b in reads:
            if b in writes:
                continue
            b.r[ev[0]] = ev

    def op(self, k, reads, writes, fn):
        self._wait(k, self._deps(reads, writes))
        ins = fn(self.eng[k])
        if self.cnt[k] >= SEM_EPOCH:
            self._new_sem(k)
        self.cnt[k] += 1
        ins.then_inc(self.sem[k], 1)
        ev = (self.key[k], self.sem[k], self.cnt[k])
        self._record(ev, reads, writes)

    def dma(self, k, sbuf, reads, writes, fn):
        if k not in sbuf.d:
            sbuf.d[k] = [self._mk("d" + k + sbuf.name), 0, ("dma", self.nsem)]
        d = sbuf.d[k]
        self._wait(k, self._deps(reads, writes))
        inss = fn(self.eng[k])
        if not isinstance(inss, (list, tuple)):
            inss = [inss]
        for ins in inss:
            ins.then_inc(d[0], 16)
            d[1] += 16
        ev = (d[2], d[0], d[1])
        self._record(ev, reads, writes)
        return ev


def build_program():
    nc = bass.Bass("TRN2", target_bir_lowering=False)
    es = contextlib.ExitStack()

    def din(name, shape):
        return nc.dram_tensor(name, list(shape), F32, kind="ExternalInput").ap()

    xc = din("xc", [HALO + TOK, D])
    memc = din("memc", [256, D])
    hflag = din("hflag", [128, 8])
    w_in = din("w_in", [D, 4608])
    w_ba = din("w_ba", [512, D])
    w_bs = din("w_bs", [512, D])
    w_out = din("w_out", [D, D])
    w_xq = din("w_xq", [D, D])
    w_xkv = din("w_xkv", [D, 2 * D])
    w_xo = din("w_xo", [D, D])
    w_fi = din("w_fi", [D, 2 * DFF])
    w_fo = din("w_fo", [DFF, D])
    gT_d = din("gT", [128, 32])
    gfin_d = din("gfin", [D])
    b3_d = din("b3", [128, 1024])
    b4_d = din("b4", [128, 1024])
    bfar_d = din("bfar", [128, 1024])
    mask4_d = din("mask4", [128, 1024])
    sgwT_d = din("sgwT", [128, 1024])
    sgmT_d = din("sgmT", [128, 1024])
    sgwN_d = din("sgwN", [128, 1024])
    sgmN_d = din("sgmN", [128, 1024])
    sgbT_d = din("sgbT", [128, 8])
    lng_d = din("lng", [512])
    lnb_d = din("lnb", [512])
    ident_d = din("ident", [128, 128])
    out = nc.dram_tensor("out", [TOK, D], F32, kind="ExternalOutput").ap()
    NUNIT = 40
    wscr = nc.dram_tensor("wscr", [NUNIT, 128, 4096], BF16, kind="Internal").ap()

    S = Sched(nc, es)

    def sb(name, shape, dt):
        return es.enter_context(nc.sbuf_tensor("s_" + name, list(shape), dt))

    with es:
        ring = [sb(f"ring{i}", [128, 4096], BF16) for i in range(NSLOT)]
        ringB = [Buf(f"ring{i}") for i in range(NSLOT)]
        xs_sets = [[sb(f"xs{j}_{i}", [128, D], F32) for i in range(4)] for j in range(2)]
        xsB_sets = [[Buf(f"xs{j}_{i}") for i in range(4)] for j in range(2)]
        xs = list(xs_sets[0])
        xsB = list(xsB_sets[0])

        def use_set(j):
            xs[:] = xs_sets[j]
            xsB[:] = xsB_sets[j]
        hT = sb("hT", [128, 8 * T], BF16)
        hT3 = hT[:].rearrange("p (k t) -> p k t", k=8)
        hTB = [Buf(f"hT{i}") for i in range(4)]
        hn = [sb(f"hn{i}", [128, D], BF16) for i in range(2)]
        hnB = [Buf(f"hn{i}") for i in range(2)]
        junk = sb("junk", [128, D], BF16)
        junkB = Buf("junk")
        st_ss = sb("st_ss", [128, 8], F32)
        st_sd = sb("st_sd", [128, 8], F32)
        st_rs = sb("st_rs", [128, 8], F32)
        ssB = [Buf(f"ss{i}") for i in range(8)]
        sdB = [Buf(f"sd{i}") for i in range(8)]
        rsB = [Buf(f"rs{i}") for i in range(8)]
        qT = sb("qT", [128, 4 * T], BF16)
        qT3 = qT[:].rearrange("p (a t) -> p a t", a=4)
        qTB = [Buf(f"qT{i}") for i in range(4)]
        kT = sb("kT", [128, 4 * 2 * T], BF16)
        kT3 = kT[:].rearrange("p (a t) -> p a t", a=4)
        kTB = [[Buf(f"kT{h}_{a}") for a in range(4)] for h in range(2)]
        vA = sb("vA", [128, 8 * 528], BF16)
        vA3 = vA[:].rearrange("p (s c) -> p s c", s=8)
        vAB = [Buf(f"vA{i}") for i in range(8)]
        arena = sb("arena", [128, 13312], BF16)
        gu = [arena[:, 5120 + 1024 * i: 5120 + 1024 * (i + 1)].bitcast(F32) for i in range(4)]
        guB = [Buf(f"gu{i}") for i in range(4)]
        gv = [sb(f"gv{i}", [128, 512], F32) for i in range(2)]
        gvB = [Buf(f"gv{i}") for i in range(2)]
        sq = sb("sq", [128, 512], F32)
        sqB = Buf("sq")
        lst = sb("lst", [128, 64], F32)
        lstB = [Buf(f"lst{i}") for i in range(2)]
        vn = [sb(f"vn{i}", [128, 512], BF16) for i in range(2)]
        vnB = [Buf(f"vn{i}") for i in range(2)]
        pT = [arena[:, 2560 * i: 2560 * (i + 1)] for i in range(2)]
        pT3 = [t.rearrange("p (j c) -> p j c", j=5) for t in pT]
        pTB = [[Buf(f"pT{i}_{j}") for j in range(5)] for i in range(2)]
        rc = sb("rc", [128, 8], F32)
        rcB = [Buf(f"rc{i}") for i in range(2)]
        yatt = [sb(f"yatt{i}", [128, 512], BF16) for i in range(2)]
        yattB = [Buf(f"yatt{i}") for i in range(2)]
        ysg = [sb(f"ysg{i}", [128, 512], BF16) for i in range(2)]
        ysgB = [Buf(f"ysg{i}") for i in range(2)]
        svt, svtB = gv, gvB
        yTa = sb("yTa", [128, 4 * T], BF16)
        yTa3 = yTa[:].rearrange("p (k t) -> p k t", k=4)
        yTaB = [Buf(f"yTa{i}") for i in range(4)]
        yTs = sb("yTs", [128, 4 * T], BF16)
        yTs3 = yTs[:].rearrange("p (k t) -> p k t", k=4)
        yTsB = [Buf(f"yTs{i}") for i in range(4)]
        gt1 = [arena[:, 9216 + 1024 * i: 9216 + 1024 * (i + 1)].bitcast(F32) for i in range(2)]
        gt1B = [Buf(f"gt1_{i}") for i in range(2)]
        gt2 = [arena[:, 11264 + 1024 * i: 11264 + 1024 * (i + 1)].bitcast(F32) for i in range(2)]
        gt2B = [Buf(f"gt2_{i}") for i in range(2)]
        mT = sb("mT", [128, 8 * T], BF16)
        mT3 = mT[:].rearrange("p (k t) -> p k t", k=8)
        mTB = [Buf(f"mT{i}") for i in range(8)]
        qxT = sb("qxT", [128, 8 * T], BF16)
        qxT3 = qxT[:].rearrange("p (k t) -> p k t", k=8)
        qxTB = [Buf(f"qxT{i}") for i in range(8)]
        pxT = [sb(f"pxT{i}", [128, 2 * T], BF16) for i in range(2)]
        pxT3 = [t[:].rearrange("p (m t) -> p m t", m=2) for t in pxT]
        pxTB = [[Buf(f"pxT{i}_{m}") for m in range(2)] for i in range(2)]
        rden, rdenB = gv, gvB
        kxT = sb("kxT", [128, 8 * 256], BF16)
        kxT3 = kxT[:].rearrange("p (k m) -> p k m", k=8)
        kxTB = Buf("kxT")
        vx = sb("vx", [128, 2 * D], BF16)
        vx3 = vx[:].rearrange("p (m c) -> p m c", m=2)
        vxB = Buf("vx")
        def actc(c):
            return arena[:, c * 512:(c + 1) * 512]
        actTB = [Buf(f"actT{i}") for i in range(22)]
        sil, silB = gv, gvB
        ost = [yTa[:].bitcast(F32), yTs[:].bitcast(F32)]
        ostB = [Buf(f"ost{i}") for i in range(2)]
        alias([ostB[0]], yTaB)
        alias([ostB[1]], yTsB)
        alias(actTB, [b for l in pTB for b in l] + guB + gt1B + gt2B)
        E3 = sb("E3", [128, 1024], F32)
        E4 = sb("E4", [128, 1024], F32)
        ctmp = qxT[:, 0:2048].bitcast(F32)
        ctmp2 = qxT[:, 2048:4096].bitcast(F32)
        gfin = sb("gfin", [128, D], F32)
        gTs = sb("gTs", [128, 32], F32)
        lgB = sb("lgB", [128, 512], F32)
        lbB = junk[:].bitcast(F32)
        Ctab = sb("Ctab", [128, 512], F32)
        WsT = sb("WsT", [128, 1024], BF16)
        wsum = sb("wsum", [128, 8], F32)
        sgbT = sb("sgbT", [128, 8], F32)
        hfl = sb("hfl", [128, 8], F32)
        epsc = sb("epsc", [128, 8], F32)
        ident = sb("ident", [128, 128], BF16)
        ones = sb("ones", [128, 128], BF16)
        constB = Buf("const")
        identB = Buf("ident")
        E3B, E4B, ctmpB, ctmp2B, CtabB, WsTB, wsumB, onesB = (Buf(n) for n in
                                                             ("E3", "E4", "ctmp", "ctmp2", "Ctab", "WsT", "wsum", "ones"))
        alias([ctmpB, ctmp2B], qxTB)

        PS = [es.enter_context(nc.psum_tensor(f"ps{i}", [128, 512], F32)) for i in range(8)]
        PSB = [Buf(f"ps{i}") for i in range(8)]
        rr = [0]

        def nb():
            i = rr[0] % 8
            rr[0] += 1
            return PS[i], PSB[i]

        def v3(ap2, k):
            return ap2.rearrange("p (k c) -> p k c", k=k)

        def wview(w):
            return w.rearrange("(k p) c -> p k c", p=128)

        units = {}
        uid = [0]

        def defunit(name, parts):
            units[name] = (uid[0], parts)
            uid[0] += 1

        wi = wview(w_in)
        for i, nm in enumerate(["q", "k", "v", "u", "vs", "ga0", "ga1", "gb0", "gb1"]):
            defunit(nm, [(lambda s: v3(s, 8), wi[:, :, i * 512:(i + 1) * 512])])
        defunit("wa", [(lambda s: v3(s, 4), wview(w_ba))])
        defunit("wb", [(lambda s: v3(s, 4), wview(w_bs))])
        for nm, w in (("wo", w_out), ("xq", w_xq), ("xo", w_xo)):
            for h in range(2):
                defunit(f"{nm}{h}", [(lambda s: v3(s, 8), wview(w)[:, :, h * 512:(h + 1) * 512])])
        wf = wview(w_fi)
        for j in range(11):
            defunit(f"fi{j}", [
                (lambda s: v3(s, 8)[:, :, 0:256], wf[:, :, 256 * j:256 * j + 256]),
                (lambda s: v3(s, 8)[:, :, 256:512], wf[:, :, DFF + 256 * j:DFF + 256 * j + 256]),
            ])
        wfo = wview(w_fo)
        for j in range(6):
            nk = 4 if j < 5 else 2
            defunit(f"fo{j}", [(lambda s, nk=nk: v3(s, 4)[:, 0:nk, :], wfo[:, 4 * j:4 * j + nk, :])])
        wkv = wview(w_xkv)
        for i in range(4):
            defunit(f"xkv{i}", [(lambda s: v3(s, 8), wkv[:, :, i * 512:(i + 1) * 512])])
        assert uid[0] <= NUNIT
        scrB = {}
        in_scr = set()

        pass_units = (["u", "vs", "q", "k", "v", "wa", "wb", "ga0", "gb0", "ga1", "gb1", "wo0", "wo1",
                       "xq0", "xq1", "xo0", "xo1"] + [f"fi{j}" for j in range(11)] + [f"fo{j}" for j in range(6)])
        seq = ["xkv0", "xkv1", "xkv2", "xkv3", "k", "v"]
        for _ in range(NPASS):
            seq += pass_units
        st = {"loaded": 0, "done": [False] * len(seq), "pos": 0}

        def emit_load(i):
            name = seq[i]
            u, parts = units[name]
            s = i % NSLOT
            slot = ring[s][:]
            if name in in_scr:
                S.dma("sp", ringB[s], [scrB[name]], [ringB[s]],
                      lambda e: e.dma_start(out=slot, in_=wscr[u]))
            else:
                S.dma("pool", ringB[s], [], [ringB[s]],
                      lambda e: [e.dma_start(out=f(slot), in_=src) for f, src in parts])
                if not name.startswith("xkv"):
                    scrB[name] = Buf("scr" + name)
                    S.dma("sp", ringB[s], [ringB[s]], [scrB[name]],
                          lambda e: e.dma_start(out=wscr[u], in_=slot))
                    in_scr.add(name)

        def pump():
            while st["loaded"] < len(seq):
                i = st["loaded"]
                if i >= NSLOT and not st["done"][i - NSLOT]:
                    break
                emit_load(i)
                st["loaded"] += 1

        def acquire(name):
            i = st["pos"]
            while seq[i] != name or st["done"][i]:
                i += 1
            pump()
            assert i < st["loaded"], (name, i, st["loaded"])
            s = i % NSLOT
            return i, ring[s][:], ringB[s]

        def release(i):
            st["done"][i] = True
            while st["pos"] < len(seq) and st["done"][st["pos"]]:
                st["pos"] += 1
            pump()

        def ck(name):
            MARKS.append((name, S.eng["pe"].n))
            if STOP == name:
                raise _Stop()

        def body():
            pump()
            for tt in range(2):
                S.dma("sp", xsB_sets[0][tt], [], [xsB_sets[0][tt]],
                      lambda e, tt=tt: e.dma_start(out=xs_sets[0][tt][:], in_=memc[tt * 128:(tt + 1) * 128, :]))
            def cload(dst, src, eng="sp"):
                S.dma(eng, constB, [], [], lambda e: e.dma_start(out=dst, in_=src))

            cload(ctmp[:], b3_d)
            cload(ctmp2[:], bfar_d)
            cload(E4[:], b4_d)
            cload(E3[:], mask4_d)
            cload(gfin[:], gfin_d.partition_broadcast(128))
            cload(gTs[:], gT_d)
            cload(lgB[:], lng_d.partition_broadcast(128))
            S.dma("sp", junkB, [], [junkB], lambda e: e.dma_start(out=lbB, in_=lnb_d.partition_broadcast(128)))
            cload(sgbT[:], sgbT_d)
            cload(hfl[:], hflag)
            cload(ident[:], ident_d, "pool")
            for tt in range(4):
                S.dma("sp", xsB_sets[1][tt], [], [xsB_sets[1][tt]],
                      lambda e, tt=tt: e.dma_start(out=xs_sets[1][tt][:], in_=xc[tt * 128:(tt + 1) * 128, :]))
            for tt in (2, 3):
                S.dma("sp", xsB_sets[0][tt], [], [xsB_sets[0][tt]],
                      lambda e, tt=tt: e.dma_start(out=xs_sets[0][tt][:], in_=xc[HALO + tt * 128:HALO + (tt + 1) * 128, :]))
            ck("c0")
            constB.w = constB.d["sp"][2], constB.d["sp"][0], constB.d["sp"][1]
            identB.w = constB.d["pool"][2], constB.d["pool"][0], constB.d["pool"][1]
            S.op("pool", [], [onesB], lambda e: e.memset(ones[:], 1.0))
            S.op("pool", [], [onesB], lambda e: e.memset(epsc[:], EPS))
            S.op("dve", [constB], [E4B], lambda e: e.tensor_tensor(out=E4[:], in0=E4[:], in1=ctmp2[:], op=ALU.subtract))
            S.op("act", [E4B], [E4B], lambda e: e.activation(out=E4[:], in_=E4[:], func=AF.Exp))
            S.op("dve", [E4B, constB], [E4B], lambda e: e.tensor_tensor(out=E4[:], in0=E4[:], in1=E3[:], op=ALU.mult))
            S.op("dve", [constB, E4B], [E3B], lambda e: e.tensor_tensor(out=E3[:], in0=ctmp[:], in1=ctmp2[:], op=ALU.subtract))
            S.op("act", [E3B], [E3B], lambda e: e.activation(out=E3[:], in_=E3[:], func=AF.Exp))
            ck("c1")
            S.dma("sp", ctmpB, [E3B], [ctmpB], lambda e: e.dma_start(out=ctmp[:], in_=sgwT_d))
            S.dma("sp", ctmp2B, [E3B], [ctmp2B], lambda e: e.dma_start(out=ctmp2[:], in_=sgmT_d))
            S.op("dve", [ctmpB, ctmp2B], [WsTB], lambda e: e.tensor_tensor(out=WsT[:], in0=ctmp[:], in1=ctmp2[:], op=ALU.mult))
            S.dma("sp", ctmpB, [], [ctmpB], lambda e: e.dma_start(out=ctmp[:], in_=sgwN_d))
            S.dma("sp", ctmp2B, [], [ctmp2B], lambda e: e.dma_start(out=ctmp2[:], in_=sgmN_d))
            S.op("dve", [ctmpB, ctmp2B], [ctmpB], lambda e: e.tensor_tensor(out=ctmp[:], in0=ctmp[:], in1=ctmp2[:], op=ALU.mult))
            S.op("dve", [ctmpB], [wsumB], lambda e: e.tensor_reduce(
                out=wsum[:], in_=ctmp[:].rearrange("p (g s) -> p g s", g=8), axis=AX.X, op=ALU.add))
            S.op("dve", [wsumB, constB, junkB], [CtabB], lambda e: e.tensor_tensor(
                out=Ctab[:].rearrange("p (g d) -> p g d", g=8), in0=lbB.rearrange("p (g d) -> p g d", g=8),
                in1=wsum[:].unsqueeze(2).to_broadcast([128, 8, 64]), op=ALU.mult))
            S.op("dve", [CtabB, constB], [CtabB], lambda e: e.tensor_tensor(
                out=Ctab[:].rearrange("p (g d) -> p g d", g=8), in0=Ctab[:].rearrange("p (g d) -> p g d", g=8),
                in1=sgbT[:].unsqueeze(2).to_broadcast([128, 8, 64]), op=ALU.add))
            ck("c2")
            for s_ in range(8):
                col = vA3[:, s_, :].rearrange("p (h e) -> p h e", e=66)[:, :, 64:65]
                if s_ < 4:
                    S.op("dve", [constB], [vAB[s_]], lambda e, col=col: e.tensor_copy(
                        out=col, in_=hfl[:].unsqueeze(2)))
                else:
                    S.op("pool", [], [vAB[s_]], lambda e, col=col: e.memset(col, 1.0))

            ck("const")
            def load_x(tt, row0):
                S.dma("sp", xsB[tt], [], [xsB[tt]],
                      lambda e: e.dma_start(out=xs[tt][:], in_=xc[row0 + tt * 128: row0 + tt * 128 + 128, :]))

            def rms_stats(tt, c=None):
                c = tt if c is None else c
                S.op("act", [xsB[tt]], [junkB, ssB[c]], lambda e: e.activation(
                    out=junk[:], in_=xs[tt][:], func=AF.Square, accum_out=st_ss[:, c:c + 1]))
                S.op("act", [ssB[c], onesB], [sdB[c]], lambda e: e.activation(
                    out=st_sd[:, c:c + 1], in_=st_ss[:, c:c + 1], func=AF.Ln, scale=1.0 / D, bias=epsc[:, 0:1]))
                S.op("act", [sdB[c]], [rsB[c]], lambda e: e.activation(
                    out=st_rs[:, c:c + 1], in_=st_sd[:, c:c + 1], func=AF.Exp, scale=-0.5))

            def norm_pre(tt):
                rms_stats(tt)
                hb = tt % 2
                S.op("act", [xsB[tt], rsB[tt]], [hnB[hb]], lambda e: e.activation(
                    out=hn[hb][:], in_=xs[tt][:], func=AF.Copy, scale=st_rs[:, tt:tt + 1]))

            def norm_tr(tt, gidx, bi=None):
                hb = tt % 2
                bank, bB = nb() if bi is None else (PS[bi], PSB[bi])
                bv = bank[:].bitcast(BF16)

                def tr(e):
                    for k in range(8):
                        ins = e.transpose(out=bv[:, k * 128:(k + 1) * 128], in_=hn[hb][:, k * 128:(k + 1) * 128],
                                          identity=ident[:])
                    return ins
                S.op("pe", [hnB[hb], identB], [bB], tr)
                S.op("dve", [bB, constB], [hTB[tt]], lambda e: e.tensor_tensor(
                    out=hT3[:, :, tt * 128:(tt + 1) * 128], in0=bv.rearrange("p (k t) -> p k t", k=8),
                    in1=gTs[:, gidx * 8:(gidx + 1) * 8].unsqueeze(2).to_broadcast([128, 8, 128]), op=ALU.mult))

            def norm_T(tts, gidx):
                for tt in tts:
                    norm_pre(tt)
                    norm_tr(tt, gidx)

            def proj_fm(slot, sB, c0, src3, srcBs, nk, ntok, kslice=None):
                bank, bB = nb()

                def f(e):
                    for k in range(nk):
                        ins = e.matmul(bank[:, 0:ntok], lhsT=slot[:, k, c0:c0 + 128], rhs=src3[:, k, 0:ntok],
                                       start=(k == 0), stop=(k == nk - 1))
                    return ins
                S.op("pe", [sB] + srcBs, [bB], f)
                return bank, bB

            def proj_fm_split(slot, sB, c0, src3, nk):
                bank, bB = nb()

                def half(hf):
                    def f(e):
                        for k in range(nk):
                            ins = e.matmul(bank[:, hf * 256:(hf + 1) * 256], lhsT=slot[:, k, c0:c0 + 128],
                                           rhs=src3[:, k, hf * 256:(hf + 1) * 256], start=(k == 0), stop=(k == nk - 1))
                        return ins
                    S.op("pe", [sB, hTB[2 * hf], hTB[2 * hf + 1]], [bB], f)
                half(0)
                return bank, bB, (lambda: half(1))

            def proj_tm(slot, sB, c0, src3, srcB, nk, tt, bank=None, bB=None, first=True, last=True, k0=0):
                if bank is None:
                    bank, bB = nb()

                def f(e):
                    for k in range(nk):
                        ins = e.matmul(bank[:], lhsT=src3[:, k0 + k, tt * 128:(tt + 1) * 128], rhs=slot[:, k, c0:c0 + 512],
                                       start=(first and k == 0), stop=(last and k == nk - 1))
                    return ins
                S.op("pe", [sB] + srcB, [bB], f)
                return bank, bB

            norm_T([0, 1], 2)
            for tt in (0, 1):
                load_x(tt, HALO)
            use_set(1)
            norm_pre(0)
            norm_pre(1)
            use_set(0)
            for i in range(2):
                ui, slot, sB = acquire(f"xkv{i}")
                s3 = v3(slot, 8)
                for ff in range(4):
                    f_ = i * 4 + ff
                    bank, bB = proj_fm(s3, sB, ff * 128, hT3, [hTB[0], hTB[1]], 8, 256)
                    S.op("act", [bB], [kxTB], lambda e, bank=bank, f_=f_: e.activation(
                        out=kxT3[:, f_, :], in_=bank[:, 0:256], func=AF.Copy))
                release(ui)
            for i in range(2):
                ui, slot, sB = acquire(f"xkv{2 + i}")
                s3 = v3(slot, 8)
                for mt in range(2):
                    bank, bB = proj_tm(s3, sB, 0, hT3, [hTB[mt]], 8, mt)
                    S.op("act", [bB], [vxB], lambda e, bank=bank, mt=mt, i=i: e.activation(
                        out=vx3[:, mt, i * 512:(i + 1) * 512], in_=bank[:], func=AF.Copy))
                release(ui)

            ck("kvpro")
            def k_items(p):
                half = (p + 1) % 2
                ui, slot, sB = acquire("k")
                s3 = v3(slot, 8)

                def item(a):
                    bank, bB = proj_fm(s3, sB, a * 128, hT3, hTB, 8, T)
                    S.op("act", [bB], [kTB[half][a]], lambda e: e.activation(
                        out=kT3[:, a, half * T:(half + 1) * T], in_=bank[:], func=AF.Copy))
                    if a == 3:
                        release(ui)
                return item

            def v_items(p):
                ui, slot, sB = acquire("v")
                s3 = v3(slot, 8)

                def item(tt):
                    s_ = (4 * p + tt + 4) % 8
                    bank, bB = proj_tm(s3, sB, 0, hT3, [hTB[tt]], 8, tt)
                    S.op("act", [bB], [vAB[s_]], lambda e: e.activation(
                        out=vA3[:, s_, :].rearrange("p (h e) -> p h e", e=66)[:, :, 0:64],
                        in_=bank[:].rearrange("p (h d) -> p h d", d=64), func=AF.Copy))
                    if tt == 3:
                        release(ui)
                return item

            def kv_proj(p):
                ki = k_items(p)
                for a in range(4):
                    ki(a)
                vi = v_items(p)
                for tt in range(4):
                    vi(tt)

            use_set(1)
            norm_tr(0, 0)
            norm_tr(1, 0)
            norm_pre(2)
            norm_pre(3)
            norm_tr(2, 0)
            norm_tr(3, 0)
            use_set(0)
            norm_pre(0)
            norm_pre(1)
            use_set(1)
            kv_proj(-1)
            use_set(0)

            ck("halo")
            for p in range(NPASS):
                row0 = HALO + p * T
                half = (p + 1) % 2
                use_set(p % 2)
                if p == 0:
                    norm_tr(0, 0)
                    norm_tr(1, 0)
                    norm_pre(2)
                    norm_pre(3)
                    norm_tr(2, 0)
                    norm_tr(3, 0)
                if p + 1 < NPASS:
                    use_set((p + 1) % 2)
                    for tt in range(4):
                        load_x(tt, row0 + T)
                    use_set(p % 2)
                ck("start")
                ui, slot, sB = acquire("u")
                s3 = v3(slot, 8)
                for tt in range(4):
                    bank, bB = proj_tm(s3, sB, 0, hT3, [hTB[tt]], 8, tt)
                    S.op("act", [bB], [guB[tt]], lambda e, bank=bank, tt=tt: e.activation(
                        out=gu[tt][:], in_=bank[:], func=AF.Gelu_apprx_tanh))
                release(ui)
                uvs, slot_vs, sB_vs = acquire("vs")
                s3vs = v3(slot_vs, 8)

                def sg1(tt):
                    b2 = tt % 2
                    bank, bB = proj_tm(s3vs, sB_vs, 0, hT3, [hTB[tt]], 8, tt)
                    S.op("act", [bB], [gvB[b2]], lambda e: e.activation(out=gv[b2][:], in_=bank[:], func=AF.Gelu_apprx_tanh))
                    g3 = gv[b2][:].rearrange("p (g d) -> p g d", g=8)
                    L = lst[:, b2 * 32:(b2 + 1) * 32]
                    S.op("dve", [gvB[b2]], [lstB[b2]], lambda e: e.tensor_reduce(
                        out=L[:, 0:8], in_=g3, axis=AX.X, op=ALU.add))
                    S.op("dve", [gvB[b2]], [sqB], lambda e: e.tensor_tensor(
                        out=sq[:], in0=gv[b2][:], in1=gv[b2][:], op=ALU.mult))
                    S.op("dve", [sqB, lstB[b2]], [lstB[b2]], lambda e: e.tensor_reduce(
                        out=L[:, 8:16], in_=sq[:].rearrange("p (g d) -> p g d", g=8), axis=AX.X, op=ALU.add))
                    S.op("dve", [lstB[b2]], [lstB[b2]], lambda e: e.tensor_scalar(
                        out=L[:, 16:24], in0=L[:, 0:8], scalar1=1.0 / 64, scalar2=None, op0=ALU.mult))
                    S.op("dve", [lstB[b2]], [lstB[b2]], lambda e: e.tensor_tensor(
                        out=L[:, 0:8], in0=L[:, 16:24], in1=L[:, 16:24], op=ALU.mult))
                    S.op("dve", [lstB[b2]], [lstB[b2]], lambda e: e.scalar_tensor_tensor(
                        out=L[:, 24:32], in0=L[:, 8:16], scalar=1.0 / 64, in1=L[:, 0:8], op0=ALU.mult, op1=ALU.subtract))
                    if tt == 3:
                        release(uvs)

                def sg1b(tt):
                    b2 = tt % 2
                    g3 = gv[b2][:].rearrange("p (g d) -> p g d", g=8)
                    L = lst[:, b2 * 32:(b2 + 1) * 32]
                    S.op("act", [lstB[b2], onesB], [lstB[b2]], lambda e: e.activation(
                        out=L[:, 24:32], in_=L[:, 24:32], func=AF.Ln, bias=epsc[:, 0:1]))
                    S.op("act", [lstB[b2]], [lstB[b2]], lambda e: e.activation(
                        out=L[:, 24:32], in_=L[:, 24:32], func=AF.Exp, scale=-0.5))
                    S.op("dve", [lstB[b2], gvB[b2]], [gvB[b2]], lambda e: e.tensor_tensor(
                        out=g3, in0=g3, in1=L[:, 16:24].unsqueeze(2).to_broadcast([128, 8, 64]), op=ALU.subtract))
                    S.op("dve", [lstB[b2], gvB[b2]], [vnB[b2]], lambda e: e.tensor_tensor(
                        out=vn[b2][:].rearrange("p (g d) -> p g d", g=8), in0=g3,
                        in1=L[:, 24:32].unsqueeze(2).to_broadcast([128, 8, 64]), op=ALU.mult))

                ck("u")
                uq, slot_q, sB_q = acquire("q")
                s3q = v3(slot_q, 8)

                def qi(a):
                    bank, bB = proj_fm(s3q, sB_q, a * 128, hT3, hTB, 8, T)
                    S.op("act", [bB], [qTB[a]], lambda e: e.activation(
                        out=qT3[:, a, :], in_=bank[:], func=AF.Copy, scale=0.125))
                    if a == 3:
                        release(uq)
                ki = k_items(p)
                vi = v_items(p)

                def sg2(tt):
                    b2 = tt % 2
                    bank2, bB2 = nb()

                    def sgmm(e):
                        for g in range(8):
                            ins = e.matmul(bank2[:, g * 64:(g + 1) * 64], lhsT=WsT[:, g * 128:(g + 1) * 128],
                                           rhs=vn[b2][:, g * 64:(g + 1) * 64], start=True, stop=True)
                        return ins
                    S.op("pe", [vnB[b2], WsTB], [bB2], sgmm)
                    S.op("dve", [bB2, constB], [svtB[b2]], lambda e: e.tensor_tensor(
                        out=svt[b2][:], in0=bank2[:], in1=lgB[:], op=ALU.mult))
                    S.op("dve", [svtB[b2], CtabB], [svtB[b2]], lambda e: e.tensor_tensor(
                        out=svt[b2][:], in0=svt[b2][:], in1=Ctab[:], op=ALU.add))
                    S.op("dve", [svtB[b2], guB[tt]], [ysgB[b2]], lambda e: e.tensor_tensor(
                        out=ysg[b2][:], in0=svt[b2][:], in1=gu[tt][:], op=ALU.mult))

                def sg3(tt):
                    b2 = tt % 2
                    bank3, bB3 = nb()
                    bv3 = bank3[:].bitcast(BF16)

                    def tr2(e):
                        for k in range(4):
                            ins = e.transpose(out=bv3[:, k * 128:(k + 1) * 128], in_=ysg[b2][:, k * 128:(k + 1) * 128],
                                              identity=ident[:])
                        return ins
                    S.op("pe", [ysgB[b2], identB], [bB3], tr2)
                    S.op("dve", [bB3], [yTsB[tt]], lambda e: e.tensor_copy(
                        out=yTs3[:, :, tt * 128:(tt + 1) * 128], in_=bv3[:, 0:512].rearrange("p (k t) -> p k t", k=4)))

                def attA(i):
                    qp, hg, pb = i // 2, i % 2, i % 2
                    gtile = 4 * p + qp
                    slots = [(gtile + j) % 8 for j in range(5)]
                    for j in range(5):
                        s_ = slots[j]
                        kh = s_ // 4
                        bank, bB = nb()

                        def sc(e, bank=bank, s_=s_):
                            for hh in range(4):
                                a, hp_ = hh, hg
                                ins = e.matmul(bank[:, hh * 128:(hh + 1) * 128],
                                               lhsT=kT3[hp_ * 64:(hp_ + 1) * 64, a, s_ * 128:(s_ + 1) * 128],
                                               rhs=qT3[hp_ * 64:(hp_ + 1) * 64, a, qp * 128:(qp + 1) * 128],
                                               start=True, stop=True)
                            return ins
                        S.op("pe", qTB + kTB[kh], [bB], sc)
                        S.op("act", [bB], [pTB[pb][j]], lambda e, bank=bank, j=j: e.activation(
                            out=pT3[pb][:, j, :], in_=bank[:], func=AF.Exp))
                        if j == 0:
                            S.op("pool", [], [pTB[pb][0]], lambda e: e.memset(
                                pT[pb][0:64, 0:512].rearrange("p (h i) -> p h i", i=128)[:, :, 64:128], 0.0))
                        elif j >= 3:
                            Et, EtB = (E3, E3B) if j == 3 else (E4, E4B)
                            S.op("dve", [EtB, pTB[pb][j]], [pTB[pb][j]], lambda e, j=j, Et=Et: e.tensor_tensor(
                                out=pT3[pb][:, j, :].rearrange("p (hh i) -> p hh i", i=128),
                                in0=pT3[pb][:, j, :].rearrange("p (hh i) -> p hh i", i=128),
                                in1=Et[:].rearrange("p (hh two i) -> p two hh i", two=2, i=128)[:, hg], op=ALU.mult))

                def attB(i):
                    qp, hg, pb = i // 2, i % 2, i % 2
                    yb = qp % 2
                    gtile = 4 * p + qp
                    slots = [(gtile + j) % 8 for j in range(5)]
                    bankO, bOB = nb()

                    def pv(e):
                        for hh in range(4):
                            h = 2 * hh + hg
                            for j in range(5):
                                ins = e.matmul(bankO[:, hh * 65:(hh + 1) * 65],
                                               lhsT=pT3[pb][:, j, hh * 128:(hh + 1) * 128],
                                               rhs=vA3[:, slots[j], h * 66:h * 66 + 65],
                                               start=(j == 0), stop=(j == 4), skip_group_check=True)
                        return ins
                    S.op("pe", pTB[pb] + [vAB[s_] for s_ in slots], [bOB], pv)
                    o3 = bankO[:, 0:260].rearrange("p (h e) -> p h e", e=65)
                    rcv = rc[:, pb * 4:(pb + 1) * 4]
                    S.op("dve", [bOB], [rcB[pb]], lambda e: e.reciprocal(out=rcv.unsqueeze(2), in_=o3[:, :, 64:65]))
                    S.op("dve", [bOB, rcB[pb]], [yattB[yb]], lambda e: e.tensor_tensor(
                        out=yatt[yb][:].rearrange("p (hh two d) -> p two hh d", two=2, d=64)[:, hg],
                        in0=o3[:, :, 0:64], in1=rcv.unsqueeze(2).to_broadcast([128, 4, 64]), op=ALU.mult))
                    if hg == 1:
                        bank3, bB3 = nb()
                        bv3 = bank3[:].bitcast(BF16)

                        def tr3(e):
                            for k in range(4):
                                ins = e.transpose(out=bv3[:, k * 128:(k + 1) * 128], in_=yatt[yb][:, k * 128:(k + 1) * 128],
                                                  identity=ident[:])
                            return ins
                        S.op("pe", [yattB[yb], identB], [bB3], tr3)
                        S.op("dve", [bB3], [yTaB[qp]], lambda e: e.tensor_copy(
                            out=yTa3[:, :, qp * 128:(qp + 1) * 128], in_=bv3[:, 0:512].rearrange("p (k t) -> p k t", k=4)))

                order = [(sg1, 0), (sg1, 1), (sg1b, 0), (sg1b, 1), (qi, 0), (qi, 1), (qi, 2), (qi, 3), (sg2, 0), (ki, 0),
                         (ki, 1), (sg2, 1), (ki, 2), (ki, 3), (sg3, 0), (sg3, 1), (sg1, 2), (sg1, 3), (sg1b, 2), (sg1b, 3),
                         (vi, 0), (vi, 1), (vi, 2), (vi, 3),
                         (sg2, 2), (attA, 0), (sg2, 3), (attA, 1), (sg3, 2), (attB, 0), (sg3, 3), (attA, 2), (attB, 1),
                         (attA, 3), (attB, 2), (attA, 4), (attB, 3), (attA, 5), (attB, 4), (attA, 6), (attB, 5), (attA, 7),
                         (attB, 6), (attB, 7)]
                for fn_, arg_ in order:
                    fn_(arg_)
                ck("sg")
                if p == 0:
                    for s_ in range(4):
                        col = vA3[:, s_, :].rearrange("p (h e) -> p h e", e=66)[:, :, 64:65]
                        S.op("pool", [], [vAB[s_]], lambda e, col=col: e.memset(col, 1.0))
                ck("attn")
                ua, slA, sAB = acquire("wa")
                ub, slB, sBB = acquire("wb")
                a3, b3_ = v3(slA, 4), v3(slB, 4)
                gsl = {}

                def gacq(fh):
                    uga, slGA, sGAB = acquire(f"ga{fh}")
                    ugb, slGB, sGBB = acquire(f"gb{fh}")
                    gsl[fh] = (uga, v3(slGA, 8), sGAB, ugb, v3(slGB, 8), sGBB)

                def gG(f_):
                    fh, ff, tb = f_ // 4, f_ % 4, f_ % 2
                    if fh not in gsl:
                        gacq(fh)
                    uga, ga3, sGAB, ugb, gb3, sGBB = gsl[fh]
                    bga, bgaB = proj_fm(ga3, sGAB, ff * 128, hT3, hTB, 8, T)
                    S.op("act", [bgaB], [gt1B[tb]], lambda e: e.activation(out=gt1[tb][:], in_=bga[:], func=AF.Sigmoid))
                    bgb, bgbB = proj_fm(gb3, sGBB, ff * 128, hT3, hTB, 8, T)
                    S.op("act", [bgbB], [gt2B[tb]], lambda e: e.activation(out=gt2[tb][:], in_=bgb[:], func=AF.Sigmoid))
                    if ff == 3:
                        release(uga)
                        release(ugb)

                def gP(f_):
                    tb = f_ % 2
                    bpa, bpaB = proj_fm(a3, sAB, f_ * 128, yTa3, yTaB, 4, T)
                    S.op("dve", [gt1B[tb], bpaB], [gt1B[tb]], lambda e: e.tensor_tensor(
                        out=gt1[tb][:], in0=bpa[:], in1=gt1[tb][:], op=ALU.mult))
                    bpb, bpbB = proj_fm(b3_, sBB, f_ * 128, yTs3, yTsB, 4, T)
                    S.op("dve", [gt2B[tb], bpbB], [gt2B[tb]], lambda e: e.tensor_tensor(
                        out=gt2[tb][:], in0=bpb[:], in1=gt2[tb][:], op=ALU.mult))
                    S.op("dve", [gt1B[tb], gt2B[tb]], [mTB[f_]], lambda e: e.tensor_tensor(
                        out=mT3[:, f_, :], in0=gt1[tb][:], in1=gt2[tb][:], op=ALU.add))

                gG(0); gG(1)
                for f_ in range(8):
                    gP(f_)
                    if f_ + 2 < 8:
                        gG(f_ + 2)
                release(ua)
                release(ub)

                def resid_proj(uname, src3, srcBs, gidx, filler=None):
                    u0, sl0, sB0 = acquire(f"{uname}0")
                    u1, sl1, sB1 = acquire(f"{uname}1")
                    sl = [(v3(sl0, 8), sB0), (v3(sl1, 8), sB1)]

                    def mm(tt):
                        for hc in range(2):
                            bank, bB = proj_tm(sl[hc][0], sl[hc][1], 0, src3, srcBs, 8, tt)
                            S.op("dve", [bB, xsB[tt]], [xsB[tt]], lambda e, bank=bank, tt=tt, hc=hc: e.tensor_tensor(
                                out=xs[tt][:, hc * 512:(hc + 1) * 512], in0=bank[:], in1=xs[tt][:, hc * 512:(hc + 1) * 512],
                                op=ALU.add))
                    mm(0); norm_pre(0)
                    mm(1); norm_pre(1)
                    mm(2); norm_tr(0, gidx); norm_pre(2)
                    mm(3); norm_tr(1, gidx); norm_pre(3)
                    release(u0)
                    release(u1)
                    cont = filler() if filler is not None else None
                    norm_tr(2, gidx)
                    norm_tr(3, gidx)
                    if cont is not None:
                        cont()

                ck("gate")

                def xq_filler():
                    ui, slot, sB = acquire("xq0")
                    s3 = v3(slot, 8)
                    items = [proj_fm_split(s3, sB, ff * 128, hT3, 8) for ff in range(4)]

                    def cont():
                        for ff, (bank, bB, c2) in enumerate(items):
                            c2()
                            S.op("act", [bB], [qxTB[ff]], lambda e, bank=bank, ff=ff: e.activation(
                                out=qxT3[:, ff, :], in_=bank[:], func=AF.Copy, scale=1.0 / 16))
                        release(ui)
                    return cont
                resid_proj("wo", mT3, mTB, 1, xq_filler)
                ck("wo")
                for hc in range(1, 2):
                    ui, slot, sB = acquire(f"xq{hc}")
                    s3 = v3(slot, 8)
                    for ff in range(4):
                        f_ = hc * 4 + ff
                        bank, bB = proj_fm(s3, sB, ff * 128, hT3, hTB, 8, T)
                        S.op("act", [bB], [qxTB[f_]], lambda e, bank=bank, f_=f_: e.activation(
                            out=qxT3[:, f_, :], in_=bank[:], func=AF.Copy, scale=1.0 / 16))
                    release(ui)
                ck("xq")
                def xsc(h):
                    xb = h % 2
                    for mt in range(2):
                        bank, bB = nb()

                        def xs_(e, bank=bank, mt=mt):
                            for ee in range(2):
                                ins = e.matmul(bank[:], lhsT=kxT3[:, 2 * h + ee, mt * 128:(mt + 1) * 128],
                                               rhs=qxT3[:, 2 * h + ee, :], start=(ee == 0), stop=(ee == 1))
                            return ins
                        S.op("pe", [kxTB, qxTB[2 * h], qxTB[2 * h + 1]], [bB], xs_)
                        S.op("act", [bB], [pxTB[xb][mt]], lambda e, bank=bank, mt=mt: e.activation(
                            out=pxT3[xb][:, mt, :], in_=bank[:], func=AF.Exp))

                def xpv_(h):
                    xb = h % 2
                    bankD, bDB = nb()

                    def den(e):
                        for mt in range(2):
                            ins = e.matmul(bankD[:], lhsT=ones[:], rhs=pxT3[xb][:, mt, :], start=(mt == 0), stop=(mt == 1))
                        return ins
                    S.op("pe", pxTB[xb] + [onesB], [bDB], den)
                    S.op("act", [bDB], [rdenB[xb]], lambda e: e.activation(out=rden[xb][:], in_=bankD[:], func=AF.Ln))
                    S.op("act", [rdenB[xb]], [rdenB[xb]], lambda e: e.activation(
                        out=rden[xb][:], in_=rden[xb][:], func=AF.Exp, scale=-1.0))
                    for ee in range(2):
                        bank, bB = nb()

                        def xpv(e, bank=bank, ee=ee):
                            for mt in range(2):
                                ins = e.matmul(bank[:], lhsT=vx3[:, mt, (2 * h + ee) * 128:(2 * h + ee + 1) * 128],
                                               rhs=pxT3[xb][:, mt, :], start=(mt == 0), stop=(mt == 1))
                            return ins
                        S.op("pe", pxTB[xb] + [vxB], [bB], xpv)
                        S.op("dve", [bB, rdenB[xb]], [mTB[2 * h + ee]], lambda e, bank=bank, ee=ee: e.tensor_tensor(
                            out=mT3[:, 2 * h + ee, :], in0=bank[:], in1=rden[xb][:], op=ALU.mult))

                xsc(0); xsc(1); xpv_(0); xsc(2); xpv_(1); xsc(3); xpv_(2); xpv_(3)
                ck("xcore")
                def ffn_item_evac(c, bg, bgB, bu, buB):
                    sbuf_ = c % 2
                    S.op("act", [bgB], [silB[sbuf_]], lambda e: e.activation(out=sil[sbuf_][:], in_=bg[:], func=AF.Silu))
                    dst = actc(c)
                    S.op("dve", [silB[sbuf_], buB], [actTB[c]], lambda e: e.tensor_tensor(
                        out=dst, in0=bu[:], in1=sil[sbuf_][:], op=ALU.mult))

                def ffn_filler():
                    ui, slot, sB = acquire("fi0")
                    s3 = v3(slot, 8)
                    items = []
                    for ee in range(2):
                        items.append((ee, proj_fm_split(s3, sB, ee * 128, hT3, 8), proj_fm_split(s3, sB, 256 + ee * 128, hT3, 8)))

                    def cont():
                        for ee, (bg, bgB, cg), (bu, buB, cu) in items:
                            cg()
                            cu()
                            ffn_item_evac(ee, bg, bgB, bu, buB)
                        release(ui)
                    return cont
                resid_proj("xo", mT3, mTB, 3, ffn_filler)
                ck("xattn")
                look = p + 1 < NPASS

                def la(fn_, *a_):
                    use_set((p + 1) % 2)
                    fn_(*a_)
                    use_set(p % 2)
                for j in range(1, 11):
                    ui, slot, sB = acquire(f"fi{j}")
                    s3 = v3(slot, 8)
                    for ee in range(2):
                        c = 2 * j + ee
                        sbuf_ = c % 2
                        bg, bgB = proj_fm(s3, sB, ee * 128, hT3, hTB, 8, T)
                        bu, buB = proj_fm(s3, sB, 256 + ee * 128, hT3, hTB, 8, T)
                        S.op("act", [bgB], [silB[sbuf_]], lambda e, bg=bg, sbuf_=sbuf_: e.activation(
                            out=sil[sbuf_][:], in_=bg[:], func=AF.Silu))
                        dst = actc(c)
                        S.op("dve", [silB[sbuf_], buB], [actTB[c]], lambda e, bu=bu, sbuf_=sbuf_, dst=dst: e.tensor_tensor(
                            out=dst, in0=bu[:], in1=sil[sbuf_][:], op=ALU.mult))
                    release(ui)
                    if look and j == 5:
                        la(norm_pre, 0)
                        la(norm_pre, 1)
                if look:
                    la(norm_tr, 0, 0)
                    la(norm_tr, 1, 0)
                    la(norm_pre, 2)
                    la(norm_pre, 3)
                ck("ffnin")
                for j in range(6):
                    nk = 4 if j < 5 else 2
                    ui, slot, sB = acquire(f"fo{j}")
                    s3 = v3(slot, 4)
                    if j == 0:
                        bord = [(rr[0] + i_) % 8 for i_ in range(8)]
                    for bi in bord:
                        tt, hc = bi // 2, bi % 2
                        if True:

                            def fo(e, bi=bi, j=j, nk=nk, tt=tt, hc=hc, s3=s3):
                                for k in range(nk):
                                    c = 4 * j + k
                                    src = actc(c)
                                    ins = e.matmul(PS[bi][:], lhsT=src[:, tt * 128:(tt + 1) * 128],
                                                   rhs=s3[:, k, hc * 512:(hc + 1) * 512],
                                                   start=(j == 0 and k == 0), stop=(j == 5 and k == nk - 1))
                                return ins
                            S.op("pe", [sB] + actTB[4 * j:4 * j + nk], [PSB[bi]], fo)
                    release(ui)
                for tt in range(4):
                    for hc in range(2):
                        bi = tt * 2 + hc
                        S.op("dve", [PSB[bi], xsB[tt]], [xsB[tt]], lambda e, bi=bi, tt=tt, hc=hc: e.tensor_tensor(
                            out=xs[tt][:, hc * 512:(hc + 1) * 512], in0=PS[bi][:], in1=xs[tt][:, hc * 512:(hc + 1) * 512],
                            op=ALU.add))
                    if look and tt == 1:
                        la(norm_tr, 2, 0, 0)
                        la(norm_tr, 3, 0, 1)
                        rr[0] = 2
                ck("ffn")
                for tt in range(4):
                    rms_stats(tt, 4 + tt)
                    ob = tt % 2
                    S.op("dve", [xsB[tt], rsB[4 + tt], constB], [ostB[ob]], lambda e, tt=tt, ob=ob: e.scalar_tensor_tensor(
                        out=ost[ob][:], in0=xs[tt][:], scalar=st_rs[:, 4 + tt:5 + tt], in1=gfin[:], op0=ALU.mult, op1=ALU.mult))
                    r0 = p * T + tt * 128
                    S.dma("sp", ostB[ob], [ostB[ob]], [], lambda e, ob=ob, r0=r0: e.dma_start(
                        out=out[r0:r0 + 128, :], in_=ost[ob][:]))
                ck(f"pass{p}")
        def drain():
            bl = list(ostB)
            if STOP is not None:
                bl += ringB + xsB + [constB, ctmpB, ctmp2B]
            for b in bl:
                for d in b.d.values():
                    nc.sync.wait_ge(d[0], d[1])
            if STOP is not None:
                for k in S.eng:
                    if S.cnt[k] > 0:
                        nc.sync.wait_ge(S.sem[k], S.cnt[k])
        try:
            body()
        except _Stop:
            pass
        drain()
    return nc


_CACHE = {}


def _tables(rel_bias, sg_w, sg_b):
    rb = rel_bias[0]
    m = np.arange(128)[:, None]
    i = np.arange(128)[None, :]
    tabs = []
    for j in (3, 4):
        dist = (4 - j) * 128 + i - m
        idx = np.clip(dist, -128, 128) + 128
        tabs.append(np.ascontiguousarray(rb[:, idx].transpose(1, 0, 2)).reshape(128, 1024))
    bfar = np.ascontiguousarray(np.broadcast_to(rb[:, 256][None, :, None], (128, 8, 128))).reshape(128, 1024)
    mask4 = np.where((i // 64 == 0) & (m // 64 == 1), 0.0, 1.0).astype(np.float32)
    mask4 = np.ascontiguousarray(np.broadcast_to(mask4[:, None, :], (128, 8, 128))).reshape(128, 1024)
    w = sg_w[0]
    sgwT = np.ascontiguousarray(w.transpose(2, 0, 1)).reshape(128, 1024)
    sgwN = np.ascontiguousarray(w.transpose(1, 0, 2)).reshape(128, 1024)
    t_ = np.arange(128)
    mk = ((t_[None, :] // 64) <= (t_[:, None] // 64)).astype(np.float32)
    sgmN = np.ascontiguousarray(np.broadcast_to(mk[:, None, :], (128, 8, 128))).reshape(128, 1024)
    sgmT = np.ascontiguousarray(np.broadcast_to(mk.T[:, None, :], (128, 8, 128))).reshape(128, 1024)
    sgbT = np.ascontiguousarray(sg_b[0].T)
    return tabs[0], tabs[1], bfar, mask4, sgwT, sgmT, sgwN, sgmN, sgbT


def kernel(x, mem, norm_mix_g, w_in, rel_bias, sg_ln_g, sg_ln_b, sg_w, sg_b,
           w_branch_att, w_branch_sg, w_out, norm_xattn_g, norm_mem_g,
           w_xq, w_xkv, w_xo, norm_ffn_g, w_ffn_in, w_ffn_out, norm_final_g):
    f = lambda a: np.ascontiguousarray(np.asarray(a), dtype=np.float32)
    x = f(x); mem = f(mem)
    if "nc" not in _CACHE:
        _CACHE["nc"] = build_program()
    nc = _CACHE["nc"]
    b3, b4, bfar, mask4, sgwT, sgmT, sgwN, sgmN, sgbT = _tables(f(rel_bias), f(sg_w), f(sg_b))
    gT = np.concatenate([f(g)[0].reshape(8, 128).T for g in (norm_mix_g, norm_xattn_g, norm_mem_g, norm_ffn_g)], axis=1)
    shared = {
        "w_in": f(w_in)[0], "w_ba": f(w_branch_att)[0], "w_bs": f(w_branch_sg)[0], "w_out": f(w_out)[0],
        "w_xq": f(w_xq)[0], "w_xkv": f(w_xkv)[0], "w_xo": f(w_xo)[0], "w_fi": f(w_ffn_in)[0], "w_fo": f(w_ffn_out)[0],
        "gT": np.ascontiguousarray(gT), "gfin": f(norm_final_g), "b3": b3, "b4": b4, "bfar": bfar, "mask4": mask4,
        "sgwT": sgwT, "sgmT": sgmT, "sgwN": sgwN, "sgmN": sgmN, "sgbT": sgbT,
        "lng": f(sg_ln_g)[0].reshape(512), "lnb": f(sg_ln_b)[0].reshape(512),
        "ident": np.eye(128, dtype=np.float32),
    }
    in_maps = []
    for c in range(NCORES):
        b, hf = c // 2, c % 2
        own = x[b, hf * TOK:(hf + 1) * TOK]
        halo = x[b, hf * TOK - HALO: hf * TOK] if hf == 1 else np.zeros((HALO, D), np.float32)
        m = dict(shared)
        m["xc"] = np.ascontiguousarray(np.concatenate([halo, own], axis=0))
        m["memc"] = mem[b]
        m["hflag"] = np.full((128, 8), float(hf), np.float32)
        in_maps.append(m)
    res = run_bass_kernel_spmd(nc, in_maps, core_ids=list(range(NCORES)))
    outp = np.empty((4, 2 * TOK, D), np.float32)
    for c in range(NCORES):
        b, hf = c // 2, c % 2
        outp[b, hf * TOK:(hf + 1) * TOK] = res.results[c]["out"]
    return outp
```
